# Optimizing a Trainium2 kernel written in Bass

```python
import jax, jax.numpy as jnp
from jax import lax
import numpy as np

D_MODEL = 1024
BATCH = 8
SEQ = 2048
DEPTH = 4

N_BRANCH = 4
MIX_WIDTH = D_MODEL // 2
CONF_KERNEL = 31
POOL_WINDOWS = (2, 4, 8, 16)
POOL_GROUPS = len(POOL_WINDOWS)
POOL_GROUP_WIDTH = MIX_WIDTH // POOL_GROUPS
SC_KERNEL = 3
GMLP_CHUNK = 128
GMLP_GROUPS = 4
GMLP_GROUP_WIDTH = MIX_WIDTH // GMLP_GROUPS
D_FF = 4 * D_MODEL
PLE_DIM = 256
EPS = 1e-6

COLS_A = 2 * MIX_WIDTH
COLS_B = MIX_WIDTH
COLS_C = 3 * MIX_WIDTH
COLS_D = 2 * MIX_WIDTH
COLS_G = N_BRANCH * D_MODEL
COLS_IN = COLS_A + COLS_B + COLS_C + COLS_D + COLS_G
SPLITS = (COLS_A, COLS_A + COLS_B, COLS_A + COLS_B + COLS_C, COLS_A + COLS_B + COLS_C + COLS_D)

kernel_name = "hybrid_conv_pool_shortconv_gmlp_block"


def rms_norm(x, g):
    xf = x.astype(jnp.float32)
    y = xf * lax.rsqrt(jnp.mean(xf * xf, axis=-1, keepdims=True) + EPS)
    return (y * g.astype(jnp.float32)).astype(x.dtype)


def layer_norm(x, g, b):
    xf = x.astype(jnp.float32)
    mu = jnp.mean(xf, axis=-1, keepdims=True)
    var = jnp.mean(jnp.square(xf - mu), axis=-1, keepdims=True)
    y = (xf - mu) * lax.rsqrt(var + EPS)
    return (y * g.astype(jnp.float32) + b.astype(jnp.float32)).astype(x.dtype)


def causal_depthwise_conv(x, w):
    K, C = w.shape
    return lax.conv_general_dilated(
        x, w[:, None, :].astype(x.dtype), window_strides=(1,), padding=[(K - 1, 0)],
        dimension_numbers=('NWC', 'WIO', 'NWC'), feature_group_count=C)


def multiscale_pool(u, pool_w, pool_scale):
    S = u.shape[1]
    c = jnp.cumsum(u.astype(jnp.float32), axis=1)
    pos = jnp.arange(S, dtype=jnp.float32)[:, None] + 1.0
    outs = []
    for gi, w in enumerate(POOL_WINDOWS):
        sl = slice(gi * POOL_GROUP_WIDTH, (gi + 1) * POOL_GROUP_WIDTH)
        cg = c[:, :, sl]
        c_shift = jnp.pad(cg, ((0, 0), (w, 0), (0, 0)))[:, :S]
        mean = (cg - c_shift) / jnp.minimum(pos, float(w))
        outs.append(mean.astype(u.dtype) - u[:, :, sl])
    pooled = jnp.stack(outs, axis=2)
    mixed = jnp.einsum('bsgc,gcd->bsgd', pooled, pool_w)
    return mixed.reshape(u.shape) * pool_scale


def spatial_gating(v, ws, bs):
    B, S, _ = v.shape
    n = S // GMLP_CHUNK
    mask = jnp.tril(jnp.ones((GMLP_CHUNK, GMLP_CHUNK), dtype=bool))
    ws_m = jnp.where(mask[None], ws, jnp.zeros_like(ws))
    vc = v.reshape(B, n, GMLP_CHUNK, GMLP_GROUPS, GMLP_GROUP_WIDTH)
    out = jnp.einsum('gts,bnsgc->bntgc', ws_m, vc) + bs.T[:, :, None]
    return out.reshape(B, S, MIX_WIDTH)


def mixer_block(h, w_in, conf_dw, conf_dw_b, conf_ln_g, conf_ln_b, pool_w, pool_scale,
                sc_conv, gmlp_ln_g, gmlp_ln_b, gmlp_ws, gmlp_bs, w_branch, w_out):
    B, S, _ = h.shape
    proj = h @ w_in
    a_in, pool_in, sc_in, g_in, gate_in = jnp.split(proj, SPLITS, axis=-1)
    a, a_gate = jnp.split(a_in, 2, axis=-1)
    ya = causal_depthwise_conv(a * jax.nn.sigmoid(a_gate), conf_dw) + conf_dw_b
    ya = jax.nn.silu(layer_norm(ya, conf_ln_g, conf_ln_b))
    yb = multiscale_pool(pool_in, pool_w, pool_scale)
    bg, cg, hx = jnp.split(sc_in, 3, axis=-1)
    yc = bg * causal_depthwise_conv(cg * hx, sc_conv)
    u, v = jnp.split(g_in, 2, axis=-1)
    yd = u * spatial_gating(layer_norm(v, gmlp_ln_g, gmlp_ln_b), gmlp_ws, gmlp_bs)
    branches = jnp.stack([ya, yb, yc, yd], axis=2)
    z = jnp.einsum('bskw,kwd->bskd', branches, w_branch)
    gates = jax.nn.sigmoid(gate_in.reshape(B, S, N_BRANCH, D_MODEL))
    merged = jnp.sum(gates * z, axis=2)
    return merged @ w_out


def setup_inputs(seed: int = 0) -> dict:
    key = jax.random.key(seed)
    ks = jax.random.split(key, 26)
    f = jnp.float32
    nrm = lambda k, shape, fan: jax.random.normal(k, shape, f) * (fan ** -0.5)
    gain = lambda k, shape: 1.0 + 0.02 * jax.random.normal(k, shape, f)
    small = lambda k, shape: 0.02 * jax.random.normal(k, shape, f)
    return {
        "x": jax.random.normal(ks[0], (BATCH, SEQ, D_MODEL), f),
        "p": jax.random.normal(ks[1], (DEPTH, BATCH, SEQ, PLE_DIM), f),
        "norm_mix": gain(ks[2], (DEPTH, D_MODEL)),
        "w_in": nrm(ks[3], (DEPTH, D_MODEL, COLS_IN), D_MODEL),
        "conf_dw": nrm(ks[4], (DEPTH, CONF_KERNEL, MIX_WIDTH), CONF_KERNEL),
        "conf_dw_b": small(ks[5], (DEPTH, MIX_WIDTH)),
        "conf_ln_g": gain(ks[6], (DEPTH, MIX_WIDTH)),
        "conf_ln_b": small(ks[7], (DEPTH, MIX_WIDTH)),
        "pool_w": nrm(ks[8], (DEPTH, POOL_GROUPS, POOL_GROUP_WIDTH, POOL_GROUP_WIDTH), POOL_GROUP_WIDTH),
        "pool_scale": gain(ks[9], (DEPTH, MIX_WIDTH)),
        "sc_conv": nrm(ks[10], (DEPTH, SC_KERNEL, MIX_WIDTH), SC_KERNEL),
        "gmlp_ln_g": gain(ks[11], (DEPTH, MIX_WIDTH)),
        "gmlp_ln_b": small(ks[12], (DEPTH, MIX_WIDTH)),
        "gmlp_ws": nrm(ks[13], (DEPTH, GMLP_GROUPS, GMLP_CHUNK, GMLP_CHUNK), GMLP_CHUNK),
        "gmlp_bs": gain(ks[14], (DEPTH, GMLP_GROUPS, GMLP_CHUNK)),
        "w_branch": nrm(ks[15], (DEPTH, N_BRANCH, MIX_WIDTH, D_MODEL), MIX_WIDTH),
        "w_out": nrm(ks[16], (DEPTH, D_MODEL, D_MODEL), D_MODEL),
        "norm_mlp": gain(ks[17], (DEPTH, D_MODEL)),
        "w_up": nrm(ks[18], (DEPTH, D_MODEL, D_FF), D_MODEL),
        "w_down": nrm(ks[19], (DEPTH, D_FF, D_MODEL), D_FF),
        "norm_ple": gain(ks[20], (DEPTH, D_MODEL)),
        "w_ple": nrm(ks[21], (DEPTH, PLE_DIM, D_MODEL), PLE_DIM),
        "w_ple_gate": nrm(ks[22], (DEPTH, D_MODEL, D_MODEL), D_MODEL),
        "norm_final": gain(ks[23], (D_MODEL,)),
    }


def reference(x, p, norm_mix, w_in, conf_dw, conf_dw_b, conf_ln_g, conf_ln_b, pool_w, pool_scale,
              sc_conv, gmlp_ln_g, gmlp_ln_b, gmlp_ws, gmlp_bs, w_branch, w_out,
              norm_mlp, w_up, w_down, norm_ple, w_ple, w_ple_gate, norm_final):
    for i in range(DEPTH):
        h = rms_norm(x, norm_mix[i])
        x = x + mixer_block(h, w_in[i], conf_dw[i], conf_dw_b[i], conf_ln_g[i], conf_ln_b[i],
                            pool_w[i], pool_scale[i], sc_conv[i], gmlp_ln_g[i], gmlp_ln_b[i],
                            gmlp_ws[i], gmlp_bs[i], w_branch[i], w_out[i])
        h = rms_norm(x, norm_mlp[i])
        x = x + jnp.square(jax.nn.relu(h @ w_up[i])) @ w_down[i]
        h = rms_norm(x, norm_ple[i])
        x = x + (p[i] @ w_ple[i]) * jax.nn.sigmoid(h @ w_ple_gate[i])
    return rms_norm(x, norm_final)
```

```python
import contextlib
import numpy as np
import concourse.bass as bass
import concourse.mybir as mybir
from concourse.bass_utils import run_bass_kernel_spmd

F32 = mybir.dt.float32
BF16 = mybir.dt.bfloat16
AF = mybir.ActivationFunctionType
ALU = mybir.AluOpType

D = 1024
S = 2048
DEPTH = 4
NJ = 8
TT = 4
TW = 512
PLE = 256
EPS = 1e-6
POOL_WINDOWS = (2, 4, 8, 16)
NV = 192
NSLOT = 5
GP = 32

V_NMIX, V_NMLP, V_NPLE = 0, 8, 16
V_CB, V_CLG, V_CLB, V_PS, V_GG, V_GB = 24, 28, 32, 36, 40, 44
V_SC, V_CW, V_NF = 48, 60, 184


class Op:
    __slots__ = ("eng", "fn", "deps", "needed", "dma", "sig", "extra_wait")

    def __init__(self, eng, fn, dma=None):
        self.eng = eng
        self.fn = fn
        self.deps = []
        self.needed = False
        self.dma = dma
        self.sig = None


class Prog:
    ENGS = ("pe", "act", "dve", "pool", "sp")

    def __init__(self):
        self.ops = {e: [] for e in self.ENGS}
        self.last_w = {}
        self.readers = {}

    def op(self, eng, fn, reads=(), writes=(), dma=None, extra=()):
        o = Op(eng, fn, dma)
        writes = list(writes) + [k for k in reads if k[0] == "ps"]
        reads = [k for k in reads if k[0] != "ps"]
        deps = {}
        for k in reads:
            w = self.last_w.get(k)
            if w is not None:
                deps[id(w)] = w
        for k in writes:
            w = self.last_w.get(k)
            if w is not None:
                deps[id(w)] = w
            for r in self.readers.get(k, ()):
                deps[id(r)] = r
        for d in extra:
            deps[id(d)] = d
        for d in deps.values():
            if d is o:
                continue
            if d.eng == "pe" and eng == "pe" and d.dma is None and dma is None:
                continue
            o.deps.append(d)
            d.needed = True
        for k in reads:
            self.readers.setdefault(k, []).append(o)
        for k in writes:
            self.last_w[k] = o
            self.readers[k] = []
        self.ops[eng].append(o)
        return o


def build_program(n_layers, first_layer=0, do_final=True, dbg_steps=None):
    nc = bass.Bass("TRN2", target_bir_lowering=False)
    P = Prog()
    es = contextlib.ExitStack()

    xT_d = nc.dram_tensor("xT", [D, S], F32, kind="ExternalInput").ap()
    pT_d = nc.dram_tensor("pT", [DEPTH, PLE, S], F32, kind="ExternalInput").ap()
    w_in_d = nc.dram_tensor("w_in", [DEPTH, D, 8192], F32, kind="ExternalInput").ap()
    w_br_d = nc.dram_tensor("w_branch", [DEPTH, 4, 512, D], F32, kind="ExternalInput").ap()
    w_out_d = nc.dram_tensor("w_out", [DEPTH, D, D], F32, kind="ExternalInput").ap()
    w_up_d = nc.dram_tensor("w_up", [DEPTH, D, 4096], F32, kind="ExternalInput").ap()
    w_dn_d = nc.dram_tensor("w_down", [DEPTH, 4096, D], F32, kind="ExternalInput").ap()
    w_ple_d = nc.dram_tensor("w_ple", [DEPTH, PLE, D], F32, kind="ExternalInput").ap()
    w_pg_d = nc.dram_tensor("w_ple_gate", [DEPTH, D, D], F32, kind="ExternalInput").ap()
    vecs_d = nc.dram_tensor("vecs", [128, DEPTH * NV], F32, kind="ExternalInput").ap()
    wsT_d = nc.dram_tensor("wsT", [DEPTH, 128, 512], F32, kind="ExternalInput").ap()
    pw_d = nc.dram_tensor("poolw", [DEPTH, 128, 512], F32, kind="ExternalInput").ap()
    bs_d = nc.dram_tensor("bs", [DEPTH, 512], F32, kind="ExternalInput").ap()
    cst_d = nc.dram_tensor("consts", [128, 272], F32, kind="ExternalInput").ap()
    out_d = nc.dram_tensor("outT", [D, S], F32, kind="ExternalOutput").ap()

    def sb(name, shape, dt):
        return es.enter_context(nc.sbuf_tensor(name, shape, dt))

    xT = sb("xT_sb", [128, NJ, S], F32)
    hT = sb("hT_sb", [128, NJ, S], BF16)
    Mraw = sb("M_sb", [128, NJ * S // 2], F32)
    Mb = Mraw[:].bitcast(BF16).rearrange("p (a b) -> p a b", b=S)
    ya32 = Mraw[:].rearrange("p (a b) -> p a b", b=S)
    ring = sb("ring_sb", [128, NSLOT, 2048], BF16)
    G = sb("G_sb", [128, 4, GP + S], BF16)
    DG = sb("diag_sb", [128, 2, 8, 128], BF16)
    Traw = sb("T_sb", [128, 4128], F32)
    Tb = Traw[:].bitcast(BF16)
    TF = sb("TF_sb", [128, 3, TW], F32)
    LNR = sb("LNR_sb", [128, TW], F32)
    LNM = sb("LNM_sb", [128, TW], F32)
    TB = sb("TB_sb", [128, 2, TW], BF16)
    vecs = sb("vecs_sb", [128, DEPTH * NV], F32)
    ident = sb("ident_sb", [128, 128], BF16)
    maskb = sb("mask_sb", [128, 128], BF16)
    ones = sb("ones_sb", [128, 128], BF16)
    c512 = sb("c512_sb", [128, 128], BF16)
    rc = sb("rc_sb", [128, 16], F32)
    pw = sb("pw_sb", [128, 4, 128], BF16)
    pws = sb("pws_sb", [128, 4, 128], BF16)
    pwn = sb("pwn_sb", [128, 4, 128], BF16)
    wsm = sb("wsm_sb", [128, 4, 128], BF16)
    cst = sb("cst_sb", [128, 4, 128], F32)
    cA = sb("cA_sb", [128, 32], F32)
    cB = sb("cB_sb", [128, 32], F32)
    pb0 = sb("pb0_sb", [128, 16], BF16)
    small = sb("small_sb", [128, 16], F32)
    small2 = sb("small2_sb", [128, 16], F32)
    dstat = sb("dstat_sb", [128, 8, 16], F32)
    ps = [es.enter_context(nc.psum_tensor("ps%d" % i, [128, TW], F32)) for i in range(8)]

    state = {"bank": 0, "tf": 0, "tb": 0, "slot": 0}

    def next_bank():
        b = state["bank"]
        state["bank"] = (b + 1) % 8
        return b

    def next_tf():
        t = state["tf"]
        state["tf"] = (t + 1) % 3
        return t

    def next_tb():
        t = state["tb"]
        state["tb"] = (t + 1) % 2
        return t

    def ts(tt):
        return slice(tt * TW, (tt + 1) * TW)

    def gs(tt):
        return slice(GP + tt * TW, GP + (tt + 1) * TW)

    def Tkeys(off_b, nbytes):
        return [("T", u) for u in range(off_b // 1024, (off_b + nbytes - 1) // 1024 + 1)]

    def Mkeys_b(j, tt):
        return [("M", j * 4 + tt)]

    def Mkeys_f(c, tt):
        return [("M", c * 8 + tt * 2), ("M", c * 8 + tt * 2 + 1)]

    def vcol(l, c):
        return vecs[:, l * NV + c:l * NV + c + 1]

    def load_slab(src_ap, view_cols):
        s = state["slot"]
        state["slot"] = (s + 1) % NSLOT
        view = ring[:, s, :].rearrange("p (a b) -> p a b", b=view_cols)
        n_a = src_ap.shape[1]
        dst = view[:, 0:n_a, :]
        P.op("pool", lambda e, dst=dst, src=src_ap: e.dma_start(out=dst, in_=src),
             writes=[("ring", s)], dma="ring%d" % s)
        return s, view

    def mm_group(out_ap, pairs, reads, bank):
        def fn(e, out_ap=out_ap, pairs=pairs):
            n = len(pairs)
            ins = None
            for i, (l, r) in enumerate(pairs):
                ins = e.matmul(out_ap, lhsT=l, rhs=r, start=(i == 0), stop=(i == n - 1))
            return ins
        return P.op("pe", fn, reads=reads, writes=[("ps", bank)])

    def hkeys(tt):
        return [("h", kc, tt) for kc in range(NJ)]

    def load_consts():
        P.op("sp", lambda e: e.dma_start(out=vecs[:], in_=vecs_d[:, :]), writes=[("vecs",)], dma="vecs")
        P.op("sp", lambda e: e.dma_start(out=rc[:], in_=cst_d[:, 256:272]), writes=[("rc",)], dma="rc")
        P.op("pool", lambda e: e.dma_start(out=ident[:], in_=cst_d[:, 0:128]), writes=[("ident",)], dma="ident")
        P.op("pool", lambda e: e.dma_start(out=maskb[:], in_=cst_d[:, 128:256]), writes=[("mask",)], dma="mask")
        P.op("dve", lambda e: e.memset(ones[:], 1.0), writes=[("ones",)])
        P.op("dve", lambda e: e.memset(c512[:], 1.0 / 512), writes=[("c512",)])
        P.op("dve", lambda e: e.memset(G[:, :, 0:GP], 0.0), writes=[("Gpad",)])
        P.op("dve", lambda e: e.memset(cA[:], 0.0), writes=[("cA",)])
        P.op("dve", lambda e: e.memset(cB[:], 0.0), writes=[("cB",)])

    def load_x():
        xv = xT_d.rearrange("(j p) t -> p j t", p=128)
        for j in range(NJ):
            P.op("sp", lambda e, j=j: e.dma_start(out=xT[:, j, :], in_=xv[:, j, :]),
                 writes=[("x", j, tt) for tt in range(TT)], dma="x%d" % j)

    def rmsnorm(l, gcol, final=False):
        for tt in range(TT):
            norm_tt(l, gcol, tt, final)
            if final:
                store_tile(tt)

    def norm_hook(l, gcol, final=False):
        def hook(tt):
            if tt >= 1:
                norm_tt(l, gcol, tt - 1, final)
            if tt == TT - 1:
                norm_tt(l, gcol, TT - 1, final)
        return hook

    def norm_tt(l, gcol, tt, final=False):
        if True:
            bank = next_bank()
            for j in range(NJ):
                osq = 8192 + j * 1024
                sq = Tb[:, osq // 2:osq // 2 + TW]
                P.op("act", lambda e, j=j, tt=tt, sq=sq: e.activation(out=sq, in_=xT[:, j, ts(tt)], func=AF.Square),
                     reads=[("x", j, tt)], writes=Tkeys(osq, 1024))
                P.op("pe", lambda e, j=j, sq=sq, bank=bank: e.matmul(ps[bank][:], lhsT=ones[:], rhs=sq, start=(j == 0), stop=(j == NJ - 1)),
                     reads=Tkeys(osq, 1024) + [("ones",)], writes=[("ps", bank)])
            R, rk = (LNR, ("LNR",)) if tt % 2 == 0 else (LNM, ("LNM",))
            P.op("act", lambda e, R=R, bank=bank: e.activation(out=R[:], in_=ps[bank][:], func=AF.Sqrt, scale=1.0 / D, bias=epsc[:, 0:1]),
                 reads=[("ps", bank), ("epsc",)], writes=[rk])
            P.op("dve", lambda e, R=R: e.reciprocal(out=R[:], in_=R[:]),
                 reads=[rk], writes=[rk])
            for j in range(NJ):
                if final:
                    P.op("dve", lambda e, j=j, tt=tt, R=R: e.scalar_tensor_tensor(
                        out=xT[:, j, ts(tt)], in0=xT[:, j, ts(tt)], scalar=vcol(l, gcol + j), in1=R[:],
                        op0=ALU.mult, op1=ALU.mult),
                        reads=[("x", j, tt), rk, ("vecs",)], writes=[("x", j, tt)])
                else:
                    P.op("dve", lambda e, j=j, tt=tt, R=R: e.scalar_tensor_tensor(
                        out=hT[:, j, ts(tt)], in0=xT[:, j, ts(tt)], scalar=vcol(l, gcol + j), in1=R[:],
                        op0=ALU.mult, op1=ALU.mult),
                        reads=[("x", j, tt), rk, ("vecs",)], writes=[("h", j, tt)])

    def proj_group(view, slot, col0, tt, bank):
        pairs = [(view[:, kc, col0:col0 + 128], hT[:, kc, ts(tt)]) for kc in range(NJ)]
        return mm_group(ps[bank][:], pairs, [("ring", slot)] + hkeys(tt), bank)

    def slab_in(l, col0):
        return load_slab(w_in_d[l, :, col0:col0 + 256].rearrange("(kc p) j -> p kc j", p=128), 256)

    def branch_A(l):
        dve_taps = []

        def emit_taps(n):
            for _ in range(n):
                if dve_taps:
                    dve_taps.pop(0)()
        for cp in range(2):
            sa, va = slab_in(l, cp * 256)
            sg, vg = slab_in(l, 512 + cp * 256)
            for tt, c in [(tt, c) for tt in range(TT) for c in (2 * cp, 2 * cp + 1)]:
                bgt = next_bank()
                ba = next_bank()
                proj_group(vg, sg, (c % 2) * 128, tt, bgt)
                proj_group(va, sa, (c % 2) * 128, tt, ba)
                tf = next_tf()
                P.op("act", lambda e, tf=tf, b=bgt: e.activation(out=TF[:, tf, :], in_=ps[b][:], func=AF.Sigmoid),
                     reads=[("ps", bgt)], writes=[("TF", tf)])
                P.op("dve", lambda e, tf=tf, b=ba, c=c, tt=tt: e.tensor_tensor(out=G[:, c, gs(tt)], in0=ps[b][:], in1=TF[:, tf, :], op=ALU.mult),
                     reads=[("ps", ba), ("TF", tf)], writes=[("G", c, tt)])
                if cp == 1:
                    emit_taps(1)
            if cp == 0:
                g_rd = [("G", 0, t_) for t_ in range(TT)] + [("Gpad",)]
                y_wr = []
                for t_ in range(TT):
                    y_wr += Mkeys_f(0, t_)

                def mk_tap(k):
                    wk = vcol(l, V_CW + k)
                    src = G[:, 0, GP - 30 + k:GP - 30 + k + S]
                    if k == 0:
                        return lambda: P.op("dve", lambda e: e.tensor_scalar(
                            out=ya32[:, 0, :], in0=src, scalar1=wk, scalar2=vcol(l, V_CB + 0), op0=ALU.mult, op1=ALU.add),
                            reads=g_rd + [("vecs",)], writes=y_wr)
                    return lambda: P.op("dve", lambda e: e.scalar_tensor_tensor(
                        out=ya32[:, 0, :], in0=src, scalar=wk, in1=ya32[:, 0, :], op0=ALU.mult, op1=ALU.add),
                        reads=g_rd + [("vecs",)] + y_wr, writes=y_wr)
                for k in range(31):
                    dve_taps.append(mk_tap(k))
        def chunk0_stats():
            for tt in range(TT):
                oy = (0 * 4 + tt) * 1024
                P.op("act", lambda e, tt=tt, oy=oy: e.activation(out=Tb[:, oy // 2:oy // 2 + TW], in_=ya32[:, 0, ts(tt)], func=AF.Copy),
                     reads=Mkeys_f(0, tt), writes=Tkeys(oy, 1024))
                P.op("act", lambda e, tt=tt: e.activation(out=G[:, 0, gs(tt)], in_=ya32[:, 0, ts(tt)], func=AF.Square),
                     reads=Mkeys_f(0, tt), writes=[("G", 0, tt)])

        for c in range(1, 4):
            banks = [next_bank() for _ in range(TT)]
            for grp in range(4):
                half = grp % 2
                taps = list(range(grp * 8, min(grp * 8 + 8, 31)))
                nt = len(taps)
                k0 = taps[0]
                P.op("dve", lambda e, half=half, nt=nt, k0=k0, c=c: e.tensor_tensor(
                    out=DG[:, half, 0:nt, :],
                    in0=ident[:].unsqueeze(1).broadcast_to([128, nt, 128]),
                    in1=vecs[:, l * NV + V_CW + c * 31 + k0:l * NV + V_CW + c * 31 + k0 + nt].unsqueeze(2).broadcast_to([128, nt, 128]),
                    op=ALU.mult),
                    reads=[("ident",), ("vecs",)], writes=[("diag", half)])
                emit_taps(2)
                for tt in range(TT):
                    def fn(e, half=half, taps=taps, tt=tt, c=c, banks=banks):
                        ins = None
                        for i, k in enumerate(taps):
                            c0 = GP + tt * TW - 30 + k
                            ins = e.matmul(ps[banks[tt]][:], lhsT=DG[:, half, i, :], rhs=G[:, c, c0:c0 + TW],
                                           start=(k == 0), stop=(k == 30))
                        return ins
                    rd = [("diag", half), ("G", c, tt), ("Gpad",)]
                    if tt > 0:
                        rd.append(("G", c, tt - 1))
                    P.op("pe", fn, reads=rd, writes=[("ps", banks[tt])])
            if c == 3:
                emit_taps(31)
                chunk0_stats()
            for tt in range(TT):
                P.op("act", lambda e, c=c, tt=tt, b=banks[tt]: e.activation(
                    out=ya32[:, c, ts(tt)], in_=ps[b][:], func=AF.Identity, bias=vcol(l, V_CB + c)),
                    reads=[("ps", banks[tt]), ("vecs",)], writes=Mkeys_f(c, tt))
                oy = (c * 4 + tt) * 1024
                P.op("act", lambda e, c=c, tt=tt, b=banks[tt], oy=oy: e.activation(
                    out=Tb[:, oy // 2:oy // 2 + TW], in_=ps[b][:], func=AF.Identity, bias=vcol(l, V_CB + c)),
                    reads=[("ps", banks[tt]), ("vecs",)], writes=Tkeys(oy, 1024))
                P.op("act", lambda e, c=c, tt=tt, b=banks[tt]: e.activation(
                    out=G[:, c, gs(tt)], in_=ps[b][:], func=AF.Square, bias=vcol(l, V_CB + c)),
                    reads=[("ps", banks[tt]), ("vecs",)], writes=[("G", c, tt)])
        emit_taps(31)
        lnb = {}

        def S1(tt):
            b1 = next_bank()
            b2 = next_bank()
            lnb[tt] = (b1, b2)
            for c in range(4):
                oy = (c * 4 + tt) * 1024
                a1 = Tb[:, oy // 2:oy // 2 + TW]
                P.op("pe", lambda e, a1=a1, c=c, b1=b1: e.matmul(ps[b1][:], lhsT=c512[:], rhs=a1, start=(c == 0), stop=(c == 3)),
                     reads=Tkeys(oy, 1024) + [("c512",)], writes=[("ps", b1)])
                P.op("pe", lambda e, c=c, tt=tt, b2=b2: e.matmul(ps[b2][:], lhsT=c512[:], rhs=G[:, c, gs(tt)], start=(c == 0), stop=(c == 3)),
                     reads=[("G", c, tt), ("c512",)], writes=[("ps", b2)])

        def rbuf(tt):
            return (LNR, ("LNR",)) if tt % 2 == 0 else (LNM, ("LNM",))

        def S2(tt):
            b1, b2 = lnb[tt]
            R, rk = rbuf(tt)
            P.op("act", lambda e, b1=b1, R=R: e.activation(out=R[:], in_=ps[b1][:], func=AF.Square),
                 reads=[("ps", b1)], writes=[rk])
            P.op("dve", lambda e, b2=b2, R=R: e.tensor_tensor(out=R[:], in0=ps[b2][:], in1=R[:], op=ALU.subtract),
                 reads=[("ps", b2), rk], writes=[rk])
            P.op("act", lambda e, R=R: e.activation(out=R[:], in_=R[:], func=AF.Ln, bias=epsc[:, 0:1]),
                 reads=[rk, ("epsc",)], writes=[rk])
            P.op("act", lambda e, R=R: e.activation(out=R[:], in_=R[:], func=AF.Exp, scale=-0.5),
                 reads=[rk], writes=[rk])

        def S3(tt):
            b1, b2 = lnb[tt]
            R, rk = rbuf(tt)
            for c in range(4):
                tf = next_tf()
                P.op("dve", lambda e, tf=tf, c=c, tt=tt, b1=b1: e.tensor_tensor(out=TF[:, tf, :], in0=ya32[:, c, ts(tt)], in1=ps[b1][:], op=ALU.subtract),
                     reads=Mkeys_f(c, tt) + [("ps", b1)], writes=[("TF", tf)])
                P.op("dve", lambda e, tf=tf, R=R: e.tensor_tensor(out=TF[:, tf, :], in0=TF[:, tf, :], in1=R[:], op=ALU.mult),
                     reads=[("TF", tf), rk], writes=[("TF", tf)])
                P.op("act", lambda e, tf=tf, c=c, tt=tt: e.activation(out=G[:, c, gs(tt)], in_=TF[:, tf, :], func=AF.Silu,
                                                                     scale=vcol(l, V_CLG + c), bias=vcol(l, V_CLB + c)),
                     reads=[("TF", tf), ("vecs",)], writes=[("G", c, tt)])

        S1(0)
        S1(1)
        S2(0)
        S2(1)
        S3(0)
        S1(2)
        S2(2)
        S3(1)
        S1(3)
        S2(3)
        S3(2)
        S3(3)

    def merge(l, k):
        for jp in range(NJ // 2):
            sg, vg = slab_in(l, 4096 + k * 1024 + jp * 256)
            if jp % 2 == 0:
                sbr, vb = load_slab(w_br_d[l, k, :, (jp // 2) * 512:(jp // 2) * 512 + 512].rearrange("(kc p) j -> p kc j", p=128), 512)
            for tt, j in [(tt, j) for tt in range(TT) for j in (2 * jp, 2 * jp + 1)]:
                bgt = next_bank()
                bz = next_bank()
                proj_group(vg, sg, (j % 2) * 128, tt, bgt)
                pairs = [(vb[:, c, (j % 4) * 128:(j % 4) * 128 + 128], G[:, c, gs(tt)]) for c in range(4)]
                mm_group(ps[bz][:], pairs, [("ring", sbr)] + [("G", c, tt) for c in range(4)], bz)
                tf = next_tf()
                P.op("act", lambda e, tf=tf, b=bgt: e.activation(out=TF[:, tf, :], in_=ps[b][:], func=AF.Sigmoid),
                     reads=[("ps", bgt)], writes=[("TF", tf)])
                if k == 0:
                    P.op("dve", lambda e, tf=tf, b=bz, j=j, tt=tt: e.tensor_tensor(out=Mb[:, j, ts(tt)], in0=ps[b][:], in1=TF[:, tf, :], op=ALU.mult),
                         reads=[("ps", bz), ("TF", tf)], writes=Mkeys_b(j, tt))
                else:
                    P.op("dve", lambda e, tf=tf, b=bz: e.tensor_tensor(out=TF[:, tf, :], in0=ps[b][:], in1=TF[:, tf, :], op=ALU.mult),
                         reads=[("ps", bz), ("TF", tf)], writes=[("TF", tf)])
                    P.op("dve", lambda e, tf=tf, j=j, tt=tt: e.tensor_tensor(out=Mb[:, j, ts(tt)], in0=Mb[:, j, ts(tt)], in1=TF[:, tf, :], op=ALU.add),
                         reads=[("TF", tf)] + Mkeys_b(j, tt), writes=Mkeys_b(j, tt))

    def load_layer_params(l):
        P.op("pool", lambda e: e.dma_start(out=pw[:], in_=pw_d[l].rearrange("p (g d) -> p g d", d=128)),
             writes=[("pw",)], dma="pw")
        P.op("pool", lambda e: e.dma_start(out=wsm[:], in_=wsT_d[l].rearrange("p (g d) -> p g d", d=128)),
             writes=[("wsm",)], dma="wsm")
        P.op("sp", lambda e: e.dma_start(out=cst[:], in_=bs_d[l].rearrange("(g d) -> g d", d=128).partition_broadcast(128)),
             writes=[("cst",)], dma="cst")
        for g in range(4):
            w = POOL_WINDOWS[g]
            P.op("dve", lambda e, g=g, w=w: e.tensor_scalar(out=pws[:, g, :], in0=pw[:, g, :], scalar1=1.0 / w, scalar2=None, op0=ALU.mult),
                 reads=[("pw",)], writes=[("pws", g)])
        P.op("dve", lambda e: e.tensor_scalar(out=pwn[:], in0=pw[:], scalar1=-1.0, scalar2=None, op0=ALU.mult),
             reads=[("pw",)], writes=[("pwn",)])
        P.op("dve", lambda e: e.tensor_tensor(out=wsm[:], in0=wsm[:], in1=maskb[:].unsqueeze(1).broadcast_to([128, 4, 128]), op=ALU.mult),
             reads=[("wsm",), ("mask",)], writes=[("wsm",)])
        bank = next_bank()
        P.op("pe", lambda e, bank=bank: e.matmul(ps[bank][:], lhsT=ones[:], rhs=wsm[:].rearrange("p g d -> p (g d)"), start=True, stop=True),
             reads=[("wsm",), ("ones",)], writes=[("ps", bank)])
        for g in range(4):
            P.op("dve", lambda e, g=g, bank=bank: e.scalar_tensor_tensor(
                out=cst[:, g, :], in0=ps[bank][:, g * 128:(g + 1) * 128], scalar=vcol(l, V_GB + g), in1=cst[:, g, :],
                op0=ALU.mult, op1=ALU.add),
                reads=[("ps", bank), ("cst",), ("vecs",)], writes=[("cst",)])

    def branch_B(l):
        for g in range(4):
            w = POOL_WINDOWS[g]
            if g % 2 == 0:
                s, v = slab_in(l, 1024 + g * 128)
            ub_off = (g % 2) * 4096
            ub = Tb[:, ub_off // 2:ub_off // 2 + S]
            for tt in range(TT):
                bank = next_bank()
                proj_group(v, s, (g % 2) * 128, tt, bank)
                if tt == 0:
                    P.op("dve", lambda e, bank=bank: e.tensor_copy(out=cA[:, 16:32], in_=ps[bank][:, 0:16]),
                         reads=[("ps", bank)], writes=[("cA",)])
                P.op("act", lambda e, bank=bank, tt=tt, ub=ub: e.activation(out=ub[:, ts(tt)], in_=ps[bank][:], func=AF.Copy),
                     reads=[("ps", bank)], writes=Tkeys(ub_off + tt * 1024, 1024))
            P.op("dve", lambda e: e.tensor_tensor(out=cB[:, 16:32], in0=cA[:, 16:32], in1=cA[:, 15:31], op=ALU.add),
                 reads=[("cA",)], writes=[("cB",)])
            P.op("dve", lambda e: e.tensor_tensor(out=small[:, 0:16], in0=cB[:, 16:32], in1=cB[:, 14:30], op=ALU.add),
                 reads=[("cB",)], writes=[("small",)])
            P.op("dve", lambda e: e.tensor_copy(out=cB[:, 16:32], in_=small[:, 0:16]), reads=[("small",)], writes=[("cB",)])
            P.op("dve", lambda e: e.tensor_tensor(out=small[:, 0:16], in0=cB[:, 16:32], in1=cB[:, 12:28], op=ALU.add),
                 reads=[("cB",)], writes=[("small",)])
            P.op("dve", lambda e: e.tensor_copy(out=cB[:, 16:32], in_=small[:, 0:16]), reads=[("small",)], writes=[("cB",)])
            P.op("dve", lambda e: e.tensor_tensor(out=small[:, 0:16], in0=cB[:, 16:32], in1=cB[:, 8:24], op=ALU.add),
                 reads=[("cB",)], writes=[("small",)])
            P.op("dve", lambda e: e.tensor_copy(out=cB[:, 16:32], in_=small[:, 0:16]), reads=[("small",)], writes=[("cB",)])
            P.op("dve", lambda e, w=w: e.tensor_tensor(out=small[:, 0:16], in0=cB[:, 16:32], in1=cB[:, 16 - w:32 - w], op=ALU.subtract),
                 reads=[("cB",)], writes=[("small",)])
            P.op("dve", lambda e, w=w: e.tensor_scalar(out=small2[:, 0:16], in0=rc[:], scalar1=1.0 / w, scalar2=None, op0=ALU.max),
                 reads=[("rc",)], writes=[("small2",)])
            P.op("dve", lambda e: e.tensor_tensor(out=small[:, 0:16], in0=small[:, 0:16], in1=small2[:, 0:16], op=ALU.mult),
                 reads=[("small",), ("small2",)], writes=[("small",)])
            P.op("dve", lambda e: e.tensor_tensor(out=pb0[:], in0=small[:, 0:16], in1=cA[:, 16:32], op=ALU.subtract),
                 reads=[("small",), ("cA",)], writes=[("pb0",)])
            for tt in range(TT):
                bank = next_bank()
                n0 = 16 if tt == 0 else 0
                pairs = []
                for jj in range(w):
                    a = tt * TW + n0 - jj
                    pairs.append((pws[:, g, :], ub[:, a:a + TW - n0]))
                a = tt * TW + n0
                pairs.append((pwn[:, g, :], ub[:, a:a + TW - n0]))
                rd = [("pws", g), ("pwn",)] + Tkeys(ub_off + tt * 1024, 1024)
                if tt > 0:
                    rd += Tkeys(ub_off + (tt - 1) * 1024, 1024)
                mm_group(ps[bank][:, n0:TW], pairs, rd, bank)
                if tt == 0:
                    mm_group(ps[bank][:, 0:n0], [(pw[:, g, :], pb0[:, 0:n0])], [("pw",), ("pb0",)], bank)
                P.op("act", lambda e, bank=bank, g=g, tt=tt: e.activation(out=G[:, g, gs(tt)], in_=ps[bank][:], func=AF.Copy,
                                                                       scale=vcol(l, V_PS + g)),
                     reads=[("ps", bank), ("vecs",)], writes=[("G", g, tt)])

    def branch_C(l):
        for c in range(4):
            if c % 2 == 0:
                scg, vcg = slab_in(l, 2048 + c * 128)
                shx, vhx = slab_in(l, 2560 + c * 128)
                sbg, vbg = slab_in(l, 1536 + c * 128)
            db = c % 2
            o = db * 2064
            po = o * 4
            prod = Traw[:, o:o + 8 + S]
            P.op("dve", lambda e, prod=prod: e.memset(prod[:, 0:8], 0.0), writes=Tkeys(po, 32))
            for tt in range(TT):
                b1 = next_bank()
                b2 = next_bank()
                proj_group(vcg, scg, (c % 2) * 128, tt, b1)
                proj_group(vhx, shx, (c % 2) * 128, tt, b2)
                tf = next_tf()
                P.op("act", lambda e, tf=tf, b=b1: e.activation(out=TF[:, tf, :], in_=ps[b][:], func=AF.Copy),
                     reads=[("ps", b1)], writes=[("TF", tf)])
                P.op("dve", lambda e, tf=tf, b=b2, tt=tt, prod=prod: e.tensor_tensor(out=prod[:, 8 + tt * TW:8 + (tt + 1) * TW], in0=ps[b][:], in1=TF[:, tf, :], op=ALU.mult),
                     reads=[("ps", b2), ("TF", tf)], writes=Tkeys(po + 32 + tt * 2048, 2048))
            for tt in range(TT):
                b4 = next_bank()
                proj_group(vbg, sbg, (c % 2) * 128, tt, b4)
                tf = next_tf()
                rd = Tkeys(po + 32 + tt * 2048 - 8, 2048 + 8) + [("vecs",)]
                base = 8 + tt * TW - 2
                P.op("dve", lambda e, tf=tf, base=base, prod=prod, c=c: e.tensor_scalar(
                    out=TF[:, tf, :], in0=prod[:, base:base + TW], scalar1=vcol(l, V_SC + c * 3 + 0), scalar2=None, op0=ALU.mult),
                    reads=rd, writes=[("TF", tf)])
                for k in (1, 2):
                    P.op("dve", lambda e, tf=tf, base=base, prod=prod, c=c, k=k: e.scalar_tensor_tensor(
                        out=TF[:, tf, :], in0=prod[:, base + k:base + k + TW], scalar=vcol(l, V_SC + c * 3 + k), in1=TF[:, tf, :],
                        op0=ALU.mult, op1=ALU.add),
                        reads=rd + [("TF", tf)], writes=[("TF", tf)])
                P.op("dve", lambda e, tf=tf, b=b4, c=c, tt=tt: e.tensor_tensor(out=G[:, c, gs(tt)], in0=ps[b][:], in1=TF[:, tf, :], op=ALU.mult),
                     reads=[("ps", b4), ("TF", tf)], writes=[("G", c, tt)])

    def branch_D(l):
        sv0, vv0 = slab_in(l, 3584)
        sv1, vv1 = slab_in(l, 3584 + 256)
        d1bank = {}

        def d1_A(n):
            bank = next_bank()
            d1bank[n] = bank
            tt = n // 4
            sl = n % 8
            for half, (sv, vv) in enumerate(((sv0, vv0), (sv1, vv1))):
                pairs = [(hT[:, kc, n * 128:(n + 1) * 128], vv[:, kc, :]) for kc in range(NJ)]
                mm_group(ps[bank][:, half * 256:(half + 1) * 256], pairs, [("ring", sv)] + hkeys(tt), bank)
            P.op("dve", lambda e, bank=bank, sl=sl: e.bn_stats(out=dstat[:, sl, 0:6], in_=ps[bank][:]),
                 reads=[("ps", bank)], writes=[("dstat", sl)])
            P.op("dve", lambda e, sl=sl: e.bn_aggr(out=dstat[:, sl, 8:10], in_=dstat[:, sl, 0:6]),
                 reads=[("dstat", sl)], writes=[("dstat", sl)])
            P.op("act", lambda e, sl=sl: e.activation(out=dstat[:, sl, 9:10], in_=dstat[:, sl, 9:10], func=AF.Sqrt, bias=epsc[:, 0:1]),
                 reads=[("dstat", sl), ("epsc",)], writes=[("dstat", sl)])

        def d1_B(n):
            bank = d1bank[n]
            sl = n % 8
            P.op("dve", lambda e, sl=sl: e.reciprocal(out=dstat[:, sl, 9:10], in_=dstat[:, sl, 9:10]),
                 reads=[("dstat", sl)], writes=[("dstat", sl)])
            P.op("dve", lambda e, bank=bank, n=n, sl=sl: e.tensor_scalar(out=Tb[:, n * TW:(n + 1) * TW], in0=ps[bank][:],
                                                                      scalar1=dstat[:, sl, 8:9], scalar2=dstat[:, sl, 9:10],
                                                                      op0=ALU.subtract, op1=ALU.mult),
                 reads=[("ps", bank), ("dstat", sl)], writes=Tkeys(n * 1024, 1024))

        d1_A(0)
        d1_A(1)
        for n in range(16):
            if n + 2 < 16:
                d1_A(n + 2)
            d1_B(n)
        for g in range(4):
            if g % 2 == 0:
                su, vu = slab_in(l, 3072 + g * 128)
            for tt in range(TT):
                bu = next_bank()
                bq = next_bank()
                proj_group(vu, su, (g % 2) * 128, tt, bu)
                for i in range(4):
                    n = tt * 4 + i
                    mm_group(ps[bq][:, i * 128:(i + 1) * 128],
                             [(Tb[:, n * TW + g * 128:n * TW + (g + 1) * 128], wsm[:, g, :])],
                             Tkeys(n * 1024, 1024) + [("wsm",)], bq)
                tf = next_tf()
                P.op("dve", lambda e, tf=tf, bq=bq, g=g: e.scalar_tensor_tensor(
                    out=TF[:, tf, :].rearrange("p (a b) -> p a b", b=128),
                    in0=ps[bq][:].rearrange("p (a b) -> p a b", b=128),
                    scalar=vcol(l, V_GG + g),
                    in1=cst[:, g, :].unsqueeze(1).broadcast_to([128, 4, 128]),
                    op0=ALU.mult, op1=ALU.add),
                    reads=[("ps", bq), ("cst",), ("vecs",)], writes=[("TF", tf)])
                P.op("dve", lambda e, tf=tf, bu=bu, g=g, tt=tt: e.tensor_tensor(out=G[:, g, gs(tt)], in0=ps[bu][:], in1=TF[:, tf, :], op=ALU.mult),
                     reads=[("ps", bu), ("TF", tf)], writes=[("G", g, tt)])

    def out_proj(l, hook=None):
        slabs = [load_slab(w_out_d[l, :, jp * 256:jp * 256 + 256].rearrange("(kc p) j -> p kc j", p=128), 256)
                 for jp in range(NJ // 2)]
        for tt, j in [(tt, j) for tt in range(TT) for j in range(NJ)]:
            if True:
                s, v = slabs[j // 2]
                if hook is not None and j == 0 and tt >= 1:
                    hook(tt - 1)
                bank = next_bank()
                pairs = [(v[:, kc, (j % 2) * 128:(j % 2) * 128 + 128], Mb[:, kc, ts(tt)]) for kc in range(NJ)]
                rd = [("ring", s)]
                for kc in range(NJ):
                    rd += Mkeys_b(kc, tt)
                mm_group(ps[bank][:], pairs, rd, bank)
                P.op("dve", lambda e, bank=bank, j=j, tt=tt: e.tensor_tensor(out=xT[:, j, ts(tt)], in0=ps[bank][:], in1=xT[:, j, ts(tt)], op=ALU.add),
                     reads=[("ps", bank), ("x", j, tt)], writes=[("x", j, tt)])
        if hook is not None:
            hook(TT - 1)

    def ffn(l, hook=None):
        for fb in range(4):
            for ocp in range(NJ // 2):
                c0 = fb * 1024 + ocp * 256
                s, v = load_slab(w_up_d[l, :, c0:c0 + 256].rearrange("(kc p) j -> p kc j", p=128), 256)
                for tt, oc in [(tt, oc) for tt in range(TT) for oc in (2 * ocp, 2 * ocp + 1)]:
                    bank = next_bank()
                    proj_group(v, s, (oc % 2) * 128, tt, bank)
                    tf = next_tf()
                    P.op("act", lambda e, tf=tf, bank=bank: e.activation(out=TF[:, tf, :], in_=ps[bank][:], func=AF.Relu),
                         reads=[("ps", bank)], writes=[("TF", tf)])
                    P.op("dve", lambda e, tf=tf, bank=bank, oc=oc, tt=tt: e.tensor_tensor(out=Mb[:, oc, ts(tt)], in0=ps[bank][:], in1=TF[:, tf, :], op=ALU.mult),
                         reads=[("ps", bank), ("TF", tf)], writes=Mkeys_b(oc, tt))
            last_fb = (fb == 3 and hook is not None)
            if last_fb:
                dslabs = [load_slab(w_dn_d[l, fb * 1024:(fb + 1) * 1024, jp * 256:jp * 256 + 256].rearrange("(kc p) j -> p kc j", p=128), 256)
                          for jp in range(NJ // 2)]
                order = [(tt, j) for tt in range(TT) for j in range(NJ)]
            else:
                dslabs = {}
                order = [(tt, j) for jp in range(NJ // 2) for tt in range(TT) for j in (2 * jp, 2 * jp + 1)]
            for tt, j in order:
                if True:
                    if last_fb:
                        if j == 0 and tt >= 1:
                            hook(tt - 1)
                    elif (j // 2) not in dslabs:
                        jp = j // 2
                        dslabs[jp] = load_slab(w_dn_d[l, fb * 1024:(fb + 1) * 1024, jp * 256:jp * 256 + 256].rearrange("(kc p) j -> p kc j", p=128), 256)
                    s, v = dslabs[j // 2]
                    bank = next_bank()
                    pairs = [(v[:, kc, (j % 2) * 128:(j % 2) * 128 + 128], Mb[:, kc, ts(tt)]) for kc in range(NJ)]
                    rd = [("ring", s)]
                    for kc in range(NJ):
                        rd += Mkeys_b(kc, tt)
                    mm_group(ps[bank][:], pairs, rd, bank)
                    P.op("dve", lambda e, bank=bank, j=j, tt=tt: e.tensor_tensor(out=xT[:, j, ts(tt)], in0=ps[bank][:], in1=xT[:, j, ts(tt)], op=ALU.add),
                         reads=[("ps", bank), ("x", j, tt)], writes=[("x", j, tt)])
        if hook is not None:
            hook(TT - 1)

    def ple(l):
        pv = Tb[:, 0:2 * S].rearrange("p (a b) -> p a b", b=S)
        P.op("pool", lambda e: e.dma_start(out=pv, in_=pT_d[l].rearrange("(kc p) t -> p kc t", p=128)),
             writes=Tkeys(0, 8192), dma="pT")
        sp_, vp = load_slab(w_ple_d[l].rearrange("(kc p) j -> p kc j", p=128), 1024)
        for jp in range(NJ // 2):
            sg, vg = load_slab(w_pg_d[l, :, jp * 256:jp * 256 + 256].rearrange("(kc p) j -> p kc j", p=128), 256)
            for tt, j in [(tt, j) for tt in range(TT) for j in (2 * jp, 2 * jp + 1)]:
                bgt = next_bank()
                bp = next_bank()
                proj_group(vg, sg, (j % 2) * 128, tt, bgt)
                pairs = [(vp[:, kc, j * 128:(j + 1) * 128], pv[:, kc, ts(tt)]) for kc in range(2)]
                mm_group(ps[bp][:], pairs, [("ring", sp_)] + Tkeys(tt * 1024, 1024) + Tkeys(4096 + tt * 1024, 1024), bp)
                tf = next_tf()
                P.op("act", lambda e, tf=tf, b=bgt: e.activation(out=TF[:, tf, :], in_=ps[b][:], func=AF.Sigmoid),
                     reads=[("ps", bgt)], writes=[("TF", tf)])
                P.op("dve", lambda e, tf=tf, b=bp: e.tensor_tensor(out=TF[:, tf, :], in0=ps[b][:], in1=TF[:, tf, :], op=ALU.mult),
                     reads=[("ps", bp), ("TF", tf)], writes=[("TF", tf)])
                P.op("dve", lambda e, tf=tf, j=j, tt=tt: e.tensor_tensor(out=xT[:, j, ts(tt)], in0=xT[:, j, ts(tt)], in1=TF[:, tf, :], op=ALU.add),
                     reads=[("TF", tf), ("x", j, tt)], writes=[("x", j, tt)])

    out_ops = []

    def store_tile(tt):
        ov = out_d.rearrange("(j p) t -> p j t", p=128)
        out_ops.append(P.op("sp", lambda e, tt=tt: e.dma_start(out=ov[:, :, ts(tt)], in_=xT[:, :, ts(tt)]),
                            reads=[("x", j, tt) for j in range(NJ)], dma="o%d" % tt))

    def store_out():
        for tt in range(TT):
            if len(out_ops) <= tt:
                store_tile(tt)
        P.op("sp", None, extra=out_ops)

    epsc = sb("eps_sb", [128, 1], F32)
    P.op("dve", lambda e: e.memset(epsc[:], EPS), writes=[("epsc",)])

    load_consts()
    load_x()
    for li in range(n_layers):
        l = first_layer + li
        steps = [lambda: rmsnorm(l, V_NMIX), lambda: branch_A(l), lambda: merge(l, 0), lambda: load_layer_params(l),
                 lambda: branch_B(l), lambda: merge(l, 1), lambda: branch_C(l), lambda: merge(l, 2),
                 lambda: branch_D(l), lambda: merge(l, 3), lambda: out_proj(l, norm_hook(l, V_NMLP)),
                 lambda: ffn(l, norm_hook(l, V_NPLE)), lambda: ple(l)]
        for si, st in enumerate(steps):
            if dbg_steps is not None and si >= dbg_steps:
                break
            st()
    if do_final:
        rmsnorm(0, V_NF, final=True)
    store_out()

    sem_stack = contextlib.ExitStack()
    LIMIT = 30000
    eng_sem = {}
    eng_cnt = {}
    dma_sem = {}
    dma_cnt = {}
    nsem = [0]

    def new_sem(name):
        nsem[0] += 1
        return sem_stack.enter_context(nc.semaphore("%s_%d" % (name, nsem[0])))

    for eng in Prog.ENGS:
        for o in P.ops[eng]:
            if o.dma is not None:
                if o.dma not in dma_sem:
                    dma_sem[o.dma] = new_sem("d_" + o.dma)
                    dma_cnt[o.dma] = 0
                dma_cnt[o.dma] += 16
                o.sig = (dma_sem[o.dma], dma_cnt[o.dma])
            elif o.needed and o.fn is not None:
                if eng not in eng_sem or eng_cnt[eng] >= LIMIT:
                    eng_sem[eng] = new_sem("e_" + eng)
                    eng_cnt[eng] = 0
                eng_cnt[eng] += 1
                o.sig = (eng_sem[eng], eng_cnt[eng])

    def emit(eng_name, e):
        waited = {}
        for o in P.ops[eng_name]:
            for d in o.deps:
                sem, val = d.sig
                key = id(sem)
                if waited.get(key, 0) < val:
                    e.wait_ge(sem, val)
                    waited[key] = val
            if o.fn is None:
                continue
            ins = o.fn(e)
            if o.sig is not None:
                if o.dma is not None:
                    ins.then_inc(o.sig[0], 16)
                else:
                    ins.then_inc(o.sig[0], 1)

    with sem_stack, nc.Block() as block:
        @block.tensor
        def _(e):
            emit("pe", e)

        @block.scalar
        def _(e):
            emit("act", e)

        @block.vector
        def _(e):
            emit("dve", e)

        @block.gpsimd
        def _(e):
            emit("pool", e)

        @block.sync
        def _(e):
            emit("sp", e)
    build_program.sbuf_left = nc.sbuf_bytes_remaining
    es.close()
    return nc


def _bf(a):
    return np.ascontiguousarray(a, dtype=np.float32)


def prep_shared(inp):
    vecs = np.zeros((128, DEPTH, NV), np.float32)

    def cols(v, n):
        return np.asarray(v, np.float32).reshape(n, 128).T

    for l in range(DEPTH):
        vecs[:, l, V_NMIX:V_NMIX + 8] = cols(inp["norm_mix"][l], 8)
        vecs[:, l, V_NMLP:V_NMLP + 8] = cols(inp["norm_mlp"][l], 8)
        vecs[:, l, V_NPLE:V_NPLE + 8] = cols(inp["norm_ple"][l], 8)
        vecs[:, l, V_CB:V_CB + 4] = cols(inp["conf_dw_b"][l], 4)
        vecs[:, l, V_CLG:V_CLG + 4] = cols(inp["conf_ln_g"][l], 4)
        vecs[:, l, V_CLB:V_CLB + 4] = cols(inp["conf_ln_b"][l], 4)
        vecs[:, l, V_PS:V_PS + 4] = cols(inp["pool_scale"][l], 4)
        vecs[:, l, V_GG:V_GG + 4] = cols(inp["gmlp_ln_g"][l], 4)
        vecs[:, l, V_GB:V_GB + 4] = cols(inp["gmlp_ln_b"][l], 4)
        sc = np.asarray(inp["sc_conv"][l], np.float32)
        vecs[:, l, V_SC:V_SC + 12] = sc.T.reshape(4, 128, 3).transpose(1, 0, 2).reshape(128, 12)
        cw = np.asarray(inp["conf_dw"][l], np.float32)
        vecs[:, l, V_CW:V_CW + 124] = cw.T.reshape(4, 128, 31).transpose(1, 0, 2).reshape(128, 124)
        vecs[:, l, V_NF:V_NF + 8] = cols(inp["norm_final"], 8)
    ws = np.asarray(inp["gmlp_ws"], np.float32)
    wsT = np.ascontiguousarray(ws.transpose(0, 3, 1, 2)).reshape(DEPTH, 128, 512)
    pwt = np.asarray(inp["pool_w"], np.float32)
    poolw = np.ascontiguousarray(pwt.transpose(0, 2, 1, 3)).reshape(DEPTH, 128, 512)
    bs = np.ascontiguousarray(np.asarray(inp["gmlp_bs"], np.float32)).reshape(DEPTH, 512)
    consts = np.zeros((128, 272), np.float32)
    consts[:, 0:128] = np.eye(128, dtype=np.float32)
    consts[:, 128:256] = np.triu(np.ones((128, 128), np.float32))
    consts[:, 256:272] = (1.0 / np.arange(1, 17, dtype=np.float32))[None, :]
    return {
        "w_in": _bf(inp["w_in"]), "w_branch": _bf(inp["w_branch"]), "w_out": _bf(inp["w_out"]),
        "w_up": _bf(inp["w_up"]), "w_down": _bf(inp["w_down"]), "w_ple": _bf(inp["w_ple"]),
        "w_ple_gate": _bf(inp["w_ple_gate"]),
        "vecs": np.ascontiguousarray(vecs.reshape(128, DEPTH * NV)), "wsT": wsT, "poolw": poolw, "bs": bs,
        "consts": consts,
    }


def kernel(**inputs):
    shared = prep_shared(inputs)
    x = np.asarray(inputs["x"], np.float32)
    p = np.asarray(inputs["p"], np.float32)
    B = x.shape[0]
    in_maps = []
    for b in range(B):
        m = dict(shared)
        m["xT"] = np.ascontiguousarray(x[b].T)
        m["pT"] = np.ascontiguousarray(p[:, b].transpose(0, 2, 1))
        in_maps.append(m)
    nc = build_program(DEPTH)
    res = run_bass_kernel_spmd(nc, in_maps, core_ids=list(range(B)))
    out = np.stack([np.ascontiguousarray(res.results[b]["outT"].T) for b in range(B)], axis=0)
    return out.astype(np.float32)
```

```python
import contextlib
import numpy as np
import concourse.bass as bass
import concourse.mybir as mybir
from concourse.bass_utils import run_bass_kernel_spmd

F32 = mybir.dt.float32
BF16 = mybir.dt.bfloat16
AF = mybir.ActivationFunctionType
ALU = mybir.AluOpType

D = 1024
S = 2048
DEPTH = 4
NJ = 8
TT = 4
TW = 512
PLE = 256
EPS = 1e-6
POOL_WINDOWS = (2, 4, 8, 16)
NV = 192
NSLOT = 5
GP = 32

V_NMIX, V_NMLP, V_NPLE = 0, 8, 16
V_CB, V_CLG, V_CLB, V_PS, V_GG, V_GB = 24, 28, 32, 36, 40, 44
V_SC, V_CW, V_NF = 48, 60, 184


class Op:
    __slots__ = ("eng", "fn", "deps", "needed", "dma", "sig", "extra_wait")

    def __init__(self, eng, fn, dma=None):
        self.eng = eng
        self.fn = fn
        self.deps = []
        self.needed = False
        self.dma = dma
        self.sig = None


class Prog:
    ENGS = ("pe", "act", "dve", "pool", "sp")

    def __init__(self):
        self.ops = {e: [] for e in self.ENGS}
        self.last_w = {}
        self.readers = {}

    def op(self, eng, fn, reads=(), writes=(), dma=None, extra=()):
        o = Op(eng, fn, dma)
        writes = list(writes) + [k for k in reads if k[0] == "ps"]
        reads = [k for k in reads if k[0] != "ps"]
        deps = {}
        for k in reads:
            w = self.last_w.get(k)
            if w is not None:
                deps[id(w)] = w
        for k in writes:
            w = self.last_w.get(k)
            if w is not None:
                deps[id(w)] = w
            for r in self.readers.get(k, ()):
                deps[id(r)] = r
        for d in extra:
            deps[id(d)] = d
        for d in deps.values():
            if d is o:
                continue
            if d.eng == "pe" and eng == "pe" and d.dma is None and dma is None:
                continue
            o.deps.append(d)
            d.needed = True
        for k in reads:
            self.readers.setdefault(k, []).append(o)
        for k in writes:
            self.last_w[k] = o
            self.readers[k] = []
        self.ops[eng].append(o)
        return o


def build_program(n_layers, first_layer=0, do_final=True, dbg_steps=None):
    nc = bass.Bass("TRN2", target_bir_lowering=False)
    P = Prog()
    es = contextlib.ExitStack()

    xT_d = nc.dram_tensor("xT", [D, S], F32, kind="ExternalInput").ap()
    pT_d = nc.dram_tensor("pT", [DEPTH, PLE, S], F32, kind="ExternalInput").ap()
    w_in_d = nc.dram_tensor("w_in", [DEPTH, D, 8192], F32, kind="ExternalInput").ap()
    w_br_d = nc.dram_tensor("w_branch", [DEPTH, 4, 512, D], F32, kind="ExternalInput").ap()
    w_out_d = nc.dram_tensor("w_out", [DEPTH, D, D], F32, kind="ExternalInput").ap()
    w_up_d = nc.dram_tensor("w_up", [DEPTH, D, 4096], F32, kind="ExternalInput").ap()
    w_dn_d = nc.dram_tensor("w_down", [DEPTH, 4096, D], F32, kind="ExternalInput").ap()
    w_ple_d = nc.dram_tensor("w_ple", [DEPTH, PLE, D], F32, kind="ExternalInput").ap()
    w_pg_d = nc.dram_tensor("w_ple_gate", [DEPTH, D, D], F32, kind="ExternalInput").ap()
    vecs_d = nc.dram_tensor("vecs", [128, DEPTH * NV], F32, kind="ExternalInput").ap()
    wsT_d = nc.dram_tensor("wsT", [DEPTH, 128, 512], F32, kind="ExternalInput").ap()
    pw_d = nc.dram_tensor("poolw", [DEPTH, 128, 512], F32, kind="ExternalInput").ap()
    bs_d = nc.dram_tensor("bs", [DEPTH, 512], F32, kind="ExternalInput").ap()
    cst_d = nc.dram_tensor("consts", [128, 272], F32, kind="ExternalInput").ap()
    out_d = nc.dram_tensor("outT", [D, S], F32, kind="ExternalOutput").ap()

    def sb(name, shape, dt):
        return es.enter_context(nc.sbuf_tensor(name, shape, dt))

    xT = sb("xT_sb", [128, NJ, S], F32)
    hT = sb("hT_sb", [128, NJ, S], BF16)
    Mraw = sb("M_sb", [128, NJ * S // 2], F32)
    Mb = Mraw[:].bitcast(BF16).rearrange("p (a b) -> p a b", b=S)
    ya32 = Mraw[:].rearrange("p (a b) -> p a b", b=S)
    ring = sb("ring_sb", [128, NSLOT, 2048], BF16)
    G = sb("G_sb", [128, 4, GP + S], BF16)
    DG = sb("diag_sb", [128, 2, 8, 128], BF16)
    Traw = sb("T_sb", [128, 4128], F32)
    Tb = Traw[:].bitcast(BF16)
    TF = sb("TF_sb", [128, 3, TW], F32)
    LNR = sb("LNR_sb", [128, TW], F32)
    LNM = sb("LNM_sb", [128, TW], F32)
    TB = sb("TB_sb", [128, 2, TW], BF16)
    vecs = sb("vecs_sb", [128, DEPTH * NV], F32)
    ident = sb("ident_sb", [128, 128], BF16)
    maskb = sb("mask_sb", [128, 128], BF16)
    ones = sb("ones_sb", [128, 128], BF16)
    c512 = sb("c512_sb", [128, 128], BF16)
    rc = sb("rc_sb", [128, 16], F32)
    pw = sb("pw_sb", [128, 4, 128], BF16)
    pws = sb("pws_sb", [128, 4, 128], BF16)
    pwn = sb("pwn_sb", [128, 4, 128], BF16)
    wsm = sb("wsm_sb", [128, 4, 128], BF16)
    cst = sb("cst_sb", [128, 4, 128], F32)
    cA = sb("cA_sb", [128, 32], F32)
    cB = sb("cB_sb", [128, 32], F32)
    pb0 = sb("pb0_sb", [128, 16], BF16)
    small = sb("small_sb", [128, 16], F32)
    small2 = sb("small2_sb", [128, 16], F32)
    dstat = sb("dstat_sb", [128, 8, 16], F32)
    ps = [es.enter_context(nc.psum_tensor("ps%d" % i, [128, TW], F32)) for i in range(8)]

    state = {"bank": 0, "tf": 0, "tb": 0, "slot": 0}

    def next_bank():
        b = state["bank"]
        state["bank"] = (b + 1) % 8
        return b

    def next_tf():
        t = state["tf"]
        state["tf"] = (t + 1) % 3
        return t

    def next_tb():
        t = state["tb"]
        state["tb"] = (t + 1) % 2
        return t

    def ts(tt):
        return slice(tt * TW, (tt + 1) * TW)

    def gs(tt):
        return slice(GP + tt * TW, GP + (tt + 1) * TW)

    def Tkeys(off_b, nbytes):
        return [("T", u) for u in range(off_b // 1024, (off_b + nbytes - 1) // 1024 + 1)]

    def Mkeys_b(j, tt):
        return [("M", j * 4 + tt)]

    def Mkeys_f(c, tt):
        return [("M", c * 8 + tt * 2), ("M", c * 8 + tt * 2 + 1)]

    def vcol(l, c):
        return vecs[:, l * NV + c:l * NV + c + 1]

    def load_slab(src_ap, view_cols):
        s = state["slot"]
        state["slot"] = (s + 1) % NSLOT
        view = ring[:, s, :].rearrange("p (a b) -> p a b", b=view_cols)
        n_a = src_ap.shape[1]
        dst = view[:, 0:n_a, :]
        P.op("pool", lambda e, dst=dst, src=src_ap: e.dma_start(out=dst, in_=src),
             writes=[("ring", s)], dma="ring%d" % s)
        return s, view

    def mm_group(out_ap, pairs, reads, bank):
        def fn(e, out_ap=out_ap, pairs=pairs):
            n = len(pairs)
            ins = None
            for i, (l, r) in enumerate(pairs):
                ins = e.matmul(out_ap, lhsT=l, rhs=r, start=(i == 0), stop=(i == n - 1))
            return ins
        return P.op("pe", fn, reads=reads, writes=[("ps", bank)])

    def hkeys(tt):
        return [("h", kc, tt) for kc in range(NJ)]

    def load_consts():
        P.op("sp", lambda e: e.dma_start(out=vecs[:], in_=vecs_d[:, :]), writes=[("vecs",)], dma="vecs")
        P.op("sp", lambda e: e.dma_start(out=rc[:], in_=cst_d[:, 256:272]), writes=[("rc",)], dma="rc")
        P.op("pool", lambda e: e.dma_start(out=ident[:], in_=cst_d[:, 0:128]), writes=[("ident",)], dma="ident")
        P.op("pool", lambda e: e.dma_start(out=maskb[:], in_=cst_d[:, 128:256]), writes=[("mask",)], dma="mask")
        P.op("dve", lambda e: e.memset(ones[:], 1.0), writes=[("ones",)])
        P.op("dve", lambda e: e.memset(c512[:], 1.0 / 512), writes=[("c512",)])
        P.op("dve", lambda e: e.memset(G[:, :, 0:GP], 0.0), writes=[("Gpad",)])
        P.op("dve", lambda e: e.memset(cA[:], 0.0), writes=[("cA",)])
        P.op("dve", lambda e: e.memset(cB[:], 0.0), writes=[("cB",)])

    def load_x():
        xv = xT_d.rearrange("(j p) t -> p j t", p=128)
        for j in range(NJ):
            P.op("sp", lambda e, j=j: e.dma_start(out=xT[:, j, :], in_=xv[:, j, :]),
                 writes=[("x", j, tt) for tt in range(TT)], dma="x%d" % j)

    def rmsnorm(l, gcol, final=False):
        for tt in range(TT):
            norm_tt(l, gcol, tt, final)
            if final:
                store_tile(tt)

    def norm_hook(l, gcol, final=False):
        def hook(tt):
            if tt >= 1:
                norm_tt(l, gcol, tt - 1, final)
            if tt == TT - 1:
                norm_tt(l, gcol, TT - 1, final)
        return hook

    def norm_tt(l, gcol, tt, final=False):
        if True:
            bank = next_bank()
            for j in range(NJ):
                osq = 8192 + j * 1024
                sq = Tb[:, osq // 2:osq // 2 + TW]
                P.op("act", lambda e, j=j, tt=tt, sq=sq: e.activation(out=sq, in_=xT[:, j, ts(tt)], func=AF.Square),
                     reads=[("x", j, tt)], writes=Tkeys(osq, 1024))
                P.op("pe", lambda e, j=j, sq=sq, bank=bank: e.matmul(ps[bank][:], lhsT=ones[:], rhs=sq, start=(j == 0), stop=(j == NJ - 1)),
                     reads=Tkeys(osq, 1024) + [("ones",)], writes=[("ps", bank)])
            R, rk = (LNR, ("LNR",)) if tt % 2 == 0 else (LNM, ("LNM",))
            P.op("act", lambda e, R=R, bank=bank: e.activation(out=R[:], in_=ps[bank][:], func=AF.Sqrt, scale=1.0 / D, bias=epsc[:, 0:1]),
                 reads=[("ps", bank), ("epsc",)], writes=[rk])
            P.op("dve", lambda e, R=R: e.reciprocal(out=R[:], in_=R[:]),
                 reads=[rk], writes=[rk])
            for j in range(NJ):
                if final:
                    P.op("dve", lambda e, j=j, tt=tt, R=R: e.scalar_tensor_tensor(
                        out=xT[:, j, ts(tt)], in0=xT[:, j, ts(tt)], scalar=vcol(l, gcol + j), in1=R[:],
                        op0=ALU.mult, op1=ALU.mult),
                        reads=[("x", j, tt), rk, ("vecs",)], writes=[("x", j, tt)])
                else:
                    P.op("dve", lambda e, j=j, tt=tt, R=R: e.scalar_tensor_tensor(
                        out=hT[:, j, ts(tt)], in0=xT[:, j, ts(tt)], scalar=vcol(l, gcol + j), in1=R[:],
                        op0=ALU.mult, op1=ALU.mult),
                        reads=[("x", j, tt), rk, ("vecs",)], writes=[("h", j, tt)])

    def proj_group(view, slot, col0, tt, bank):
        pairs = [(view[:, kc, col0:col0 + 128], hT[:, kc, ts(tt)]) for kc in range(NJ)]
        return mm_group(ps[bank][:], pairs, [("ring", slot)] + hkeys(tt), bank)

    def slab_in(l, col0):
        return load_slab(w_in_d[l, :, col0:col0 + 256].rearrange("(kc p) j -> p kc j", p=128), 256)

    def branch_A(l):
        dve_taps = []

        def emit_taps(n):
            for _ in range(n):
                if dve_taps:
                    dve_taps.pop(0)()
        for cp in range(2):
            sa, va = slab_in(l, cp * 256)
            sg, vg = slab_in(l, 512 + cp * 256)
            for tt, c in [(tt, c) for tt in range(TT) for c in (2 * cp, 2 * cp + 1)]:
                bgt = next_bank()
                ba = next_bank()
                proj_group(vg, sg, (c % 2) * 128, tt, bgt)
                proj_group(va, sa, (c % 2) * 128, tt, ba)
                tf = next_tf()
                P.op("act", lambda e, tf=tf, b=bgt: e.activation(out=TF[:, tf, :], in_=ps[b][:], func=AF.Sigmoid),
                     reads=[("ps", bgt)], writes=[("TF", tf)])
                P.op("dve", lambda e, tf=tf, b=ba, c=c, tt=tt: e.tensor_tensor(out=G[:, c, gs(tt)], in0=ps[b][:], in1=TF[:, tf, :], op=ALU.mult),
                     reads=[("ps", ba), ("TF", tf)], writes=[("G", c, tt)])
                if cp == 1:
                    emit_taps(1)
            if cp == 0:
                g_rd = [("G", 0, t_) for t_ in range(TT)] + [("Gpad",)]
                y_wr = []
                for t_ in range(TT):
                    y_wr += Mkeys_f(0, t_)

                def mk_tap(k):
                    wk = vcol(l, V_CW + k)
                    src = G[:, 0, GP - 30 + k:GP - 30 + k + S]
                    if k == 0:
                        return lambda: P.op("dve", lambda e: e.tensor_scalar(
                            out=ya32[:, 0, :], in0=src, scalar1=wk, scalar2=vcol(l, V_CB + 0), op0=ALU.mult, op1=ALU.add),
                            reads=g_rd + [("vecs",)], writes=y_wr)
                    return lambda: P.op("dve", lambda e: e.scalar_tensor_tensor(
                        out=ya32[:, 0, :], in0=src, scalar=wk, in1=ya32[:, 0, :], op0=ALU.mult, op1=ALU.add),
                        reads=g_rd + [("vecs",)] + y_wr, writes=y_wr)
                for k in range(31):
                    dve_taps.append(mk_tap(k))
        def chunk0_stats():
            for tt in range(TT):
                oy = (0 * 4 + tt) * 1024
                P.op("act", lambda e, tt=tt, oy=oy: e.activation(out=Tb[:, oy // 2:oy // 2 + TW], in_=ya32[:, 0, ts(tt)], func=AF.Copy),
                     reads=Mkeys_f(0, tt), writes=Tkeys(oy, 1024))
                P.op("act", lambda e, tt=tt: e.activation(out=G[:, 0, gs(tt)], in_=ya32[:, 0, ts(tt)], func=AF.Square),
                     reads=Mkeys_f(0, tt), writes=[("G", 0, tt)])

        for c in range(1, 4):
            banks = [next_bank() for _ in range(TT)]
            for grp in range(4):
                half = grp % 2
                taps = list(range(grp * 8, min(grp * 8 + 8, 31)))
                nt = len(taps)
                k0 = taps[0]
                P.op("dve", lambda e, half=half, nt=nt, k0=k0, c=c: e.tensor_tensor(
                    out=DG[:, half, 0:nt, :],
                    in0=ident[:].unsqueeze(1).broadcast_to([128, nt, 128]),
                    in1=vecs[:, l * NV + V_CW + c * 31 + k0:l * NV + V_CW + c * 31 + k0 + nt].unsqueeze(2).broadcast_to([128, nt, 128]),
                    op=ALU.mult),
                    reads=[("ident",), ("vecs",)], writes=[("diag", half)])
                emit_taps(2)
                for tt in range(TT):
                    def fn(e, half=half, taps=taps, tt=tt, c=c, banks=banks):
                        ins = None
                        for i, k in enumerate(taps):
                            c0 = GP + tt * TW - 30 + k
                            ins = e.matmul(ps[banks[tt]][:], lhsT=DG[:, half, i, :], rhs=G[:, c, c0:c0 + TW],
                                           start=(k == 0), stop=(k == 30))
                        return ins
                    rd = [("diag", half), ("G", c, tt), ("Gpad",)]
                    if tt > 0:
                        rd.append(("G", c, tt - 1))
                    P.op("pe", fn, reads=rd, writes=[("ps", banks[tt])])
            if c == 3:
                emit_taps(31)
                chunk0_stats()
            for tt in range(TT):
                P.op("act", lambda e, c=c, tt=tt, b=banks[tt]: e.activation(
                    out=ya32[:, c, ts(tt)], in_=ps[b][:], func=AF.Identity, bias=vcol(l, V_CB + c)),
                    reads=[("ps", banks[tt]), ("vecs",)], writes=Mkeys_f(c, tt))
                oy = (c * 4 + tt) * 1024
                P.op("act", lambda e, c=c, tt=tt, b=banks[tt], oy=oy: e.activation(
                    out=Tb[:, oy // 2:oy // 2 + TW], in_=ps[b][:], func=AF.Identity, bias=vcol(l, V_CB + c)),
                    reads=[("ps", banks[tt]), ("vecs",)], writes=Tkeys(oy, 1024))
                P.op("act", lambda e, c=c, tt=tt, b=banks[tt]: e.activation(
                    out=G[:, c, gs(tt)], in_=ps[b][:], func=AF.Square, bias=vcol(l, V_CB + c)),
                    reads=[("ps", banks[tt]), ("vecs",)], writes=[("G", c, tt)])
        emit_taps(31)
        lnb = {}

        def S1(tt):
            b1 = next_bank()
            b2 = next_bank()
            lnb[tt] = (b1, b2)
            for c in range(4):
                oy = (c * 4 + tt) * 1024
                a1 = Tb[:, oy // 2:oy // 2 + TW]
                P.op("pe", lambda e, a1=a1, c=c, b1=b1: e.matmul(ps[b1][:], lhsT=c512[:], rhs=a1, start=(c == 0), stop=(c == 3)),
                     reads=Tkeys(oy, 1024) + [("c512",)], writes=[("ps", b1)])
                P.op("pe", lambda e, c=c, tt=tt, b2=b2: e.matmul(ps[b2][:], lhsT=c512[:], rhs=G[:, c, gs(tt)], start=(c == 0), stop=(c == 3)),
                     reads=[("G", c, tt), ("c512",)], writes=[("ps", b2)])

        def rbuf(tt):
            return (LNR, ("LNR",)) if tt % 2 == 0 else (LNM, ("LNM",))

        def S2(tt):
            b1, b2 = lnb[tt]
            R, rk = rbuf(tt)
            P.op("act", lambda e, b1=b1, R=R: e.activation(out=R[:], in_=ps[b1][:], func=AF.Square),
                 reads=[("ps", b1)], writes=[rk])
            P.op("dve", lambda e, b2=b2, R=R: e.tensor_tensor(out=R[:], in0=ps[b2][:], in1=R[:], op=ALU.subtract),
                 reads=[("ps", b2), rk], writes=[rk])
            P.op("act", lambda e, R=R: e.activation(out=R[:], in_=R[:], func=AF.Ln, bias=epsc[:, 0:1]),
                 reads=[rk, ("epsc",)], writes=[rk])
            P.op("act", lambda e, R=R: e.activation(out=R[:], in_=R[:], func=AF.Exp, scale=-0.5),
                 reads=[rk], writes=[rk])

        def S3(tt):
            b1, b2 = lnb[tt]
            R, rk = rbuf(tt)
            for c in range(4):
                tf = next_tf()
                P.op("dve", lambda e, tf=tf, c=c, tt=tt, b1=b1: e.tensor_tensor(out=TF[:, tf, :], in0=ya32[:, c, ts(tt)], in1=ps[b1][:], op=ALU.subtract),
                     reads=Mkeys_f(c, tt) + [("ps", b1)], writes=[("TF", tf)])
                P.op("dve", lambda e, tf=tf, R=R: e.tensor_tensor(out=TF[:, tf, :], in0=TF[:, tf, :], in1=R[:], op=ALU.mult),
                     reads=[("TF", tf), rk], writes=[("TF", tf)])
                P.op("act", lambda e, tf=tf, c=c, tt=tt: e.activation(out=G[:, c, gs(tt)], in_=TF[:, tf, :], func=AF.Silu,
                                                                     scale=vcol(l, V_CLG + c), bias=vcol(l, V_CLB + c)),
                     reads=[("TF", tf), ("vecs",)], writes=[("G", c, tt)])

        S1(0)
        S1(1)
        S1(2)
        S1(3)
        S2(0)
        S2(1)
        S3(0)
        S3(1)
        S2(2)
        S2(3)
        S3(2)
        S3(3)

    def merge(l, k):
        for jp in range(NJ // 2):
            sg, vg = slab_in(l, 4096 + k * 1024 + jp * 256)
            if jp % 2 == 0:
                sbr, vb = load_slab(w_br_d[l, k, :, (jp // 2) * 512:(jp // 2) * 512 + 512].rearrange("(kc p) j -> p kc j", p=128), 512)
            for tt, j in [(tt, j) for tt in range(TT) for j in (2 * jp, 2 * jp + 1)]:
                bgt = next_bank()
                bz = next_bank()
                proj_group(vg, sg, (j % 2) * 128, tt, bgt)
                pairs = [(vb[:, c, (j % 4) * 128:(j % 4) * 128 + 128], G[:, c, gs(tt)]) for c in range(4)]
                mm_group(ps[bz][:], pairs, [("ring", sbr)] + [("G", c, tt) for c in range(4)], bz)
                tf = next_tf()
                P.op("act", lambda e, tf=tf, b=bgt: e.activation(out=TF[:, tf, :], in_=ps[b][:], func=AF.Sigmoid),
                     reads=[("ps", bgt)], writes=[("TF", tf)])
                if k == 0:
                    P.op("dve", lambda e, tf=tf, b=bz, j=j, tt=tt: e.tensor_tensor(out=Mb[:, j, ts(tt)], in0=ps[b][:], in1=TF[:, tf, :], op=ALU.mult),
                         reads=[("ps", bz), ("TF", tf)], writes=Mkeys_b(j, tt))
                else:
                    P.op("dve", lambda e, tf=tf, b=bz: e.tensor_tensor(out=TF[:, tf, :], in0=ps[b][:], in1=TF[:, tf, :], op=ALU.mult),
                         reads=[("ps", bz), ("TF", tf)], writes=[("TF", tf)])
                    P.op("dve", lambda e, tf=tf, j=j, tt=tt: e.tensor_tensor(out=Mb[:, j, ts(tt)], in0=Mb[:, j, ts(tt)], in1=TF[:, tf, :], op=ALU.add),
                         reads=[("TF", tf)] + Mkeys_b(j, tt), writes=Mkeys_b(j, tt))

    def load_layer_params(l):
        P.op("pool", lambda e: e.dma_start(out=pw[:], in_=pw_d[l].rearrange("p (g d) -> p g d", d=128)),
             writes=[("pw",)], dma="pw")
        P.op("pool", lambda e: e.dma_start(out=wsm[:], in_=wsT_d[l].rearrange("p (g d) -> p g d", d=128)),
             writes=[("wsm",)], dma="wsm")
        P.op("sp", lambda e: e.dma_start(out=cst[:], in_=bs_d[l].rearrange("(g d) -> g d", d=128).partition_broadcast(128)),
             writes=[("cst",)], dma="cst")
        for g in range(4):
            w = POOL_WINDOWS[g]
            P.op("dve", lambda e, g=g, w=w: e.tensor_scalar(out=pws[:, g, :], in0=pw[:, g, :], scalar1=1.0 / w, scalar2=None, op0=ALU.mult),
                 reads=[("pw",)], writes=[("pws", g)])
        P.op("dve", lambda e: e.tensor_scalar(out=pwn[:], in0=pw[:], scalar1=-1.0, scalar2=None, op0=ALU.mult),
             reads=[("pw",)], writes=[("pwn",)])
        P.op("dve", lambda e: e.tensor_tensor(out=wsm[:], in0=wsm[:], in1=maskb[:].unsqueeze(1).broadcast_to([128, 4, 128]), op=ALU.mult),
             reads=[("wsm",), ("mask",)], writes=[("wsm",)])
        bank = next_bank()
        P.op("pe", lambda e, bank=bank: e.matmul(ps[bank][:], lhsT=ones[:], rhs=wsm[:].rearrange("p g d -> p (g d)"), start=True, stop=True),
             reads=[("wsm",), ("ones",)], writes=[("ps", bank)])
        for g in range(4):
            P.op("dve", lambda e, g=g, bank=bank: e.scalar_tensor_tensor(
                out=cst[:, g, :], in0=ps[bank][:, g * 128:(g + 1) * 128], scalar=vcol(l, V_GB + g), in1=cst[:, g, :],
                op0=ALU.mult, op1=ALU.add),
                reads=[("ps", bank), ("cst",), ("vecs",)], writes=[("cst",)])

    def branch_B(l):
        for g in range(4):
            w = POOL_WINDOWS[g]
            if g % 2 == 0:
                s, v = slab_in(l, 1024 + g * 128)
            ub_off = (g % 2) * 4096
            ub = Tb[:, ub_off // 2:ub_off // 2 + S]
            for tt in range(TT):
                bank = next_bank()
                proj_group(v, s, (g % 2) * 128, tt, bank)
                if tt == 0:
                    P.op("dve", lambda e, bank=bank: e.tensor_copy(out=cA[:, 16:32], in_=ps[bank][:, 0:16]),
                         reads=[("ps", bank)], writes=[("cA",)])
                P.op("act", lambda e, bank=bank, tt=tt, ub=ub: e.activation(out=ub[:, ts(tt)], in_=ps[bank][:], func=AF.Copy),
                     reads=[("ps", bank)], writes=Tkeys(ub_off + tt * 1024, 1024))
            P.op("dve", lambda e: e.tensor_tensor(out=cB[:, 16:32], in0=cA[:, 16:32], in1=cA[:, 15:31], op=ALU.add),
                 reads=[("cA",)], writes=[("cB",)])
            P.op("dve", lambda e: e.tensor_tensor(out=small[:, 0:16], in0=cB[:, 16:32], in1=cB[:, 14:30], op=ALU.add),
                 reads=[("cB",)], writes=[("small",)])
            P.op("dve", lambda e: e.tensor_copy(out=cB[:, 16:32], in_=small[:, 0:16]), reads=[("small",)], writes=[("cB",)])
            P.op("dve", lambda e: e.tensor_tensor(out=small[:, 0:16], in0=cB[:, 16:32], in1=cB[:, 12:28], op=ALU.add),
                 reads=[("cB",)], writes=[("small",)])
            P.op("dve", lambda e: e.tensor_copy(out=cB[:, 16:32], in_=small[:, 0:16]), reads=[("small",)], writes=[("cB",)])
            P.op("dve", lambda e: e.tensor_tensor(out=small[:, 0:16], in0=cB[:, 16:32], in1=cB[:, 8:24], op=ALU.add),
                 reads=[("cB",)], writes=[("small",)])
            P.op("dve", lambda e: e.tensor_copy(out=cB[:, 16:32], in_=small[:, 0:16]), reads=[("small",)], writes=[("cB",)])
            P.op("dve", lambda e, w=w: e.tensor_tensor(out=small[:, 0:16], in0=cB[:, 16:32], in1=cB[:, 16 - w:32 - w], op=ALU.subtract),
                 reads=[("cB",)], writes=[("small",)])
            P.op("dve", lambda e, w=w: e.tensor_scalar(out=small2[:, 0:16], in0=rc[:], scalar1=1.0 / w, scalar2=None, op0=ALU.max),
                 reads=[("rc",)], writes=[("small2",)])
            P.op("dve", lambda e: e.tensor_tensor(out=small[:, 0:16], in0=small[:, 0:16], in1=small2[:, 0:16], op=ALU.mult),
                 reads=[("small",), ("small2",)], writes=[("small",)])
            P.op("dve", lambda e: e.tensor_tensor(out=pb0[:], in0=small[:, 0:16], in1=cA[:, 16:32], op=ALU.subtract),
                 reads=[("small",), ("cA",)], writes=[("pb0",)])
            for tt in range(TT):
                bank = next_bank()
                n0 = 16 if tt == 0 else 0
                pairs = []
                for jj in range(w):
                    a = tt * TW + n0 - jj
                    pairs.append((pws[:, g, :], ub[:, a:a + TW - n0]))
                a = tt * TW + n0
                pairs.append((pwn[:, g, :], ub[:, a:a + TW - n0]))
                rd = [("pws", g), ("pwn",)] + Tkeys(ub_off + tt * 1024, 1024)
                if tt > 0:
                    rd += Tkeys(ub_off + (tt - 1) * 1024, 1024)
                mm_group(ps[bank][:, n0:TW], pairs, rd, bank)
                if tt == 0:
                    mm_group(ps[bank][:, 0:n0], [(pw[:, g, :], pb0[:, 0:n0])], [("pw",), ("pb0",)], bank)
                P.op("act", lambda e, bank=bank, g=g, tt=tt: e.activation(out=G[:, g, gs(tt)], in_=ps[bank][:], func=AF.Copy,
                                                                       scale=vcol(l, V_PS + g)),
                     reads=[("ps", bank), ("vecs",)], writes=[("G", g, tt)])

    def branch_C(l):
        for c in range(4):
            if c % 2 == 0:
                scg, vcg = slab_in(l, 2048 + c * 128)
                shx, vhx = slab_in(l, 2560 + c * 128)
                sbg, vbg = slab_in(l, 1536 + c * 128)
            db = c % 2
            o = db * 2064
            po = o * 4
            prod = Traw[:, o:o + 8 + S]
            P.op("dve", lambda e, prod=prod: e.memset(prod[:, 0:8], 0.0), writes=Tkeys(po, 32))
            for tt in range(TT):
                b1 = next_bank()
                b2 = next_bank()
                proj_group(vcg, scg, (c % 2) * 128, tt, b1)
                proj_group(vhx, shx, (c % 2) * 128, tt, b2)
                tf = next_tf()
                P.op("act", lambda e, tf=tf, b=b1: e.activation(out=TF[:, tf, :], in_=ps[b][:], func=AF.Copy),
                     reads=[("ps", b1)], writes=[("TF", tf)])
                P.op("dve", lambda e, tf=tf, b=b2, tt=tt, prod=prod: e.tensor_tensor(out=prod[:, 8 + tt * TW:8 + (tt + 1) * TW], in0=ps[b][:], in1=TF[:, tf, :], op=ALU.mult),
                     reads=[("ps", b2), ("TF", tf)], writes=Tkeys(po + 32 + tt * 2048, 2048))
            for tt in range(TT):
                b4 = next_bank()
                proj_group(vbg, sbg, (c % 2) * 128, tt, b4)
                tf = next_tf()
                rd = Tkeys(po + 32 + tt * 2048 - 8, 2048 + 8) + [("vecs",)]
                base = 8 + tt * TW - 2
                P.op("dve", lambda e, tf=tf, base=base, prod=prod, c=c: e.tensor_scalar(
                    out=TF[:, tf, :], in0=prod[:, base:base + TW], scalar1=vcol(l, V_SC + c * 3 + 0), scalar2=None, op0=ALU.mult),
                    reads=rd, writes=[("TF", tf)])
                for k in (1, 2):
                    P.op("dve", lambda e, tf=tf, base=base, prod=prod, c=c, k=k: e.scalar_tensor_tensor(
                        out=TF[:, tf, :], in0=prod[:, base + k:base + k + TW], scalar=vcol(l, V_SC + c * 3 + k), in1=TF[:, tf, :],
                        op0=ALU.mult, op1=ALU.add),
                        reads=rd + [("TF", tf)], writes=[("TF", tf)])
                P.op("dve", lambda e, tf=tf, b=b4, c=c, tt=tt: e.tensor_tensor(out=G[:, c, gs(tt)], in0=ps[b][:], in1=TF[:, tf, :], op=ALU.mult),
                     reads=[("ps", b4), ("TF", tf)], writes=[("G", c, tt)])

    def branch_D(l):
        sv0, vv0 = slab_in(l, 3584)
        sv1, vv1 = slab_in(l, 3584 + 256)
        d1bank = {}

        def d1_A(n):
            bank = next_bank()
            d1bank[n] = bank
            tt = n // 4
            sl = n % 8
            for half, (sv, vv) in enumerate(((sv0, vv0), (sv1, vv1))):
                pairs = [(hT[:, kc, n * 128:(n + 1) * 128], vv[:, kc, :]) for kc in range(NJ)]
                mm_group(ps[bank][:, half * 256:(half + 1) * 256], pairs, [("ring", sv)] + hkeys(tt), bank)
            P.op("dve", lambda e, bank=bank, sl=sl: e.bn_stats(out=dstat[:, sl, 0:6], in_=ps[bank][:]),
                 reads=[("ps", bank)], writes=[("dstat", sl)])
            P.op("dve", lambda e, sl=sl: e.bn_aggr(out=dstat[:, sl, 8:10], in_=dstat[:, sl, 0:6]),
                 reads=[("dstat", sl)], writes=[("dstat", sl)])
            P.op("act", lambda e, sl=sl: e.activation(out=dstat[:, sl, 9:10], in_=dstat[:, sl, 9:10], func=AF.Sqrt, bias=epsc[:, 0:1]),
                 reads=[("dstat", sl), ("epsc",)], writes=[("dstat", sl)])

        def d1_B(n):
            bank = d1bank[n]
            sl = n % 8
            P.op("dve", lambda e, sl=sl: e.reciprocal(out=dstat[:, sl, 9:10], in_=dstat[:, sl, 9:10]),
                 reads=[("dstat", sl)], writes=[("dstat", sl)])
            P.op("dve", lambda e, bank=bank, n=n, sl=sl: e.tensor_scalar(out=Tb[:, n * TW:(n + 1) * TW], in0=ps[bank][:],
                                                                      scalar1=dstat[:, sl, 8:9], scalar2=dstat[:, sl, 9:10],
                                                                      op0=ALU.subtract, op1=ALU.mult),
                 reads=[("ps", bank), ("dstat", sl)], writes=Tkeys(n * 1024, 1024))

        d1_A(0)
        for n in range(16):
            if n + 1 < 16:
                d1_A(n + 1)
            d1_B(n)
        for g in range(4):
            if g % 2 == 0:
                su, vu = slab_in(l, 3072 + g * 128)
            for tt in range(TT):
                bu = next_bank()
                bq = next_bank()
                proj_group(vu, su, (g % 2) * 128, tt, bu)
                for i in range(4):
                    n = tt * 4 + i
                    mm_group(ps[bq][:, i * 128:(i + 1) * 128],
                             [(Tb[:, n * TW + g * 128:n * TW + (g + 1) * 128], wsm[:, g, :])],
                             Tkeys(n * 1024, 1024) + [("wsm",)], bq)
                tf = next_tf()
                P.op("dve", lambda e, tf=tf, bq=bq, g=g: e.scalar_tensor_tensor(
                    out=TF[:, tf, :].rearrange("p (a b) -> p a b", b=128),
                    in0=ps[bq][:].rearrange("p (a b) -> p a b", b=128),
                    scalar=vcol(l, V_GG + g),
                    in1=cst[:, g, :].unsqueeze(1).broadcast_to([128, 4, 128]),
                    op0=ALU.mult, op1=ALU.add),
                    reads=[("ps", bq), ("cst",), ("vecs",)], writes=[("TF", tf)])
                P.op("dve", lambda e, tf=tf, bu=bu, g=g, tt=tt: e.tensor_tensor(out=G[:, g, gs(tt)], in0=ps[bu][:], in1=TF[:, tf, :], op=ALU.mult),
                     reads=[("ps", bu), ("TF", tf)], writes=[("G", g, tt)])

    def out_proj(l, hook=None):
        slabs = [load_slab(w_out_d[l, :, jp * 256:jp * 256 + 256].rearrange("(kc p) j -> p kc j", p=128), 256)
                 for jp in range(NJ // 2)]
        for tt, j in [(tt, j) for tt in range(TT) for j in range(NJ)]:
            if True:
                s, v = slabs[j // 2]
                if hook is not None and j == 0 and tt >= 1:
                    hook(tt - 1)
                bank = next_bank()
                pairs = [(v[:, kc, (j % 2) * 128:(j % 2) * 128 + 128], Mb[:, kc, ts(tt)]) for kc in range(NJ)]
                rd = [("ring", s)]
                for kc in range(NJ):
                    rd += Mkeys_b(kc, tt)
                mm_group(ps[bank][:], pairs, rd, bank)
                P.op("dve", lambda e, bank=bank, j=j, tt=tt: e.tensor_tensor(out=xT[:, j, ts(tt)], in0=ps[bank][:], in1=xT[:, j, ts(tt)], op=ALU.add),
                     reads=[("ps", bank), ("x", j, tt)], writes=[("x", j, tt)])
        if hook is not None:
            hook(TT - 1)

    def ffn(l, hook=None):
        for fb in range(4):
            for ocp in range(NJ // 2):
                c0 = fb * 1024 + ocp * 256
                s, v = load_slab(w_up_d[l, :, c0:c0 + 256].rearrange("(kc p) j -> p kc j", p=128), 256)
                for tt, oc in [(tt, oc) for tt in range(TT) for oc in (2 * ocp, 2 * ocp + 1)]:
                    bank = next_bank()
                    proj_group(v, s, (oc % 2) * 128, tt, bank)
                    tf = next_tf()
                    P.op("act", lambda e, tf=tf, bank=bank: e.activation(out=TF[:, tf, :], in_=ps[bank][:], func=AF.Relu),
                         reads=[("ps", bank)], writes=[("TF", tf)])
                    P.op("dve", lambda e, tf=tf, bank=bank, oc=oc, tt=tt: e.tensor_tensor(out=Mb[:, oc, ts(tt)], in0=ps[bank][:], in1=TF[:, tf, :], op=ALU.mult),
                         reads=[("ps", bank), ("TF", tf)], writes=Mkeys_b(oc, tt))
            last_fb = (fb == 3 and hook is not None)
            if last_fb:
                dslabs = [load_slab(w_dn_d[l, fb * 1024:(fb + 1) * 1024, jp * 256:jp * 256 + 256].rearrange("(kc p) j -> p kc j", p=128), 256)
                          for jp in range(NJ // 2)]
                order = [(tt, j) for tt in range(TT) for j in range(NJ)]
            else:
                dslabs = {}
                order = [(tt, j) for jp in range(NJ // 2) for tt in range(TT) for j in (2 * jp, 2 * jp + 1)]
            for tt, j in order:
                if True:
                    if last_fb:
                        if j == 0 and tt >= 1:
                            hook(tt - 1)
                    elif (j // 2) not in dslabs:
                        jp = j // 2
                        dslabs[jp] = load_slab(w_dn_d[l, fb * 1024:(fb + 1) * 1024, jp * 256:jp * 256 + 256].rearrange("(kc p) j -> p kc j", p=128), 256)
                    s, v = dslabs[j // 2]
                    bank = next_bank()
                    pairs = [(v[:, kc, (j % 2) * 128:(j % 2) * 128 + 128], Mb[:, kc, ts(tt)]) for kc in range(NJ)]
                    rd = [("ring", s)]
                    for kc in range(NJ):
                        rd += Mkeys_b(kc, tt)
                    mm_group(ps[bank][:], pairs, rd, bank)
                    P.op("dve", lambda e, bank=bank, j=j, tt=tt: e.tensor_tensor(out=xT[:, j, ts(tt)], in0=ps[bank][:], in1=xT[:, j, ts(tt)], op=ALU.add),
                         reads=[("ps", bank), ("x", j, tt)], writes=[("x", j, tt)])
        if hook is not None:
            hook(TT - 1)

    def ple(l):
        pv = Tb[:, 0:2 * S].rearrange("p (a b) -> p a b", b=S)
        P.op("pool", lambda e: e.dma_start(out=pv, in_=pT_d[l].rearrange("(kc p) t -> p kc t", p=128)),
             writes=Tkeys(0, 8192), dma="pT")
        sp_, vp = load_slab(w_ple_d[l].rearrange("(kc p) j -> p kc j", p=128), 1024)
        for jp in range(NJ // 2):
            sg, vg = load_slab(w_pg_d[l, :, jp * 256:jp * 256 + 256].rearrange("(kc p) j -> p kc j", p=128), 256)
            for tt, j in [(tt, j) for tt in range(TT) for j in (2 * jp, 2 * jp + 1)]:
                bgt = next_bank()
                bp = next_bank()
                proj_group(vg, sg, (j % 2) * 128, tt, bgt)
                pairs = [(vp[:, kc, j * 128:(j + 1) * 128], pv[:, kc, ts(tt)]) for kc in range(2)]
                mm_group(ps[bp][:], pairs, [("ring", sp_)] + Tkeys(tt * 1024, 1024) + Tkeys(4096 + tt * 1024, 1024), bp)
                tf = next_tf()
                P.op("act", lambda e, tf=tf, b=bgt: e.activation(out=TF[:, tf, :], in_=ps[b][:], func=AF.Sigmoid),
                     reads=[("ps", bgt)], writes=[("TF", tf)])
                P.op("dve", lambda e, tf=tf, b=bp: e.tensor_tensor(out=TF[:, tf, :], in0=ps[b][:], in1=TF[:, tf, :], op=ALU.mult),
                     reads=[("ps", bp), ("TF", tf)], writes=[("TF", tf)])
                P.op("dve", lambda e, tf=tf, j=j, tt=tt: e.tensor_tensor(out=xT[:, j, ts(tt)], in0=xT[:, j, ts(tt)], in1=TF[:, tf, :], op=ALU.add),
                     reads=[("TF", tf), ("x", j, tt)], writes=[("x", j, tt)])

    out_ops = []

    def store_tile(tt):
        ov = out_d.rearrange("(j p) t -> p j t", p=128)
        out_ops.append(P.op("sp", lambda e, tt=tt: e.dma_start(out=ov[:, :, ts(tt)], in_=xT[:, :, ts(tt)]),
                            reads=[("x", j, tt) for j in range(NJ)], dma="o%d" % tt))

    def store_out():
        for tt in range(TT):
            if len(out_ops) <= tt:
                store_tile(tt)
        P.op("sp", None, extra=out_ops)

    epsc = sb("eps_sb", [128, 1], F32)
    P.op("dve", lambda e: e.memset(epsc[:], EPS), writes=[("epsc",)])

    load_consts()
    load_x()
    for li in range(n_layers):
        l = first_layer + li
        steps = [lambda: rmsnorm(l, V_NMIX), lambda: branch_A(l), lambda: merge(l, 0), lambda: load_layer_params(l),
                 lambda: branch_B(l), lambda: merge(l, 1), lambda: branch_C(l), lambda: merge(l, 2),
                 lambda: branch_D(l), lambda: merge(l, 3), lambda: out_proj(l, norm_hook(l, V_NMLP)),
                 lambda: ffn(l, norm_hook(l, V_NPLE)), lambda: ple(l)]
        for si, st in enumerate(steps):
            if dbg_steps is not None and si >= dbg_steps:
                break
            st()
    if do_final:
        rmsnorm(0, V_NF, final=True)
    store_out()

    sem_stack = contextlib.ExitStack()
    LIMIT = 30000
    eng_sem = {}
    eng_cnt = {}
    dma_sem = {}
    dma_cnt = {}
    nsem = [0]

    def new_sem(name):
        nsem[0] += 1
        return sem_stack.enter_context(nc.semaphore("%s_%d" % (name, nsem[0])))

    for eng in Prog.ENGS:
        for o in P.ops[eng]:
            if o.dma is not None:
                if o.dma not in dma_sem:
                    dma_sem[o.dma] = new_sem("d_" + o.dma)
                    dma_cnt[o.dma] = 0
                dma_cnt[o.dma] += 16
                o.sig = (dma_sem[o.dma], dma_cnt[o.dma])
            elif o.needed and o.fn is not None:
                if eng not in eng_sem or eng_cnt[eng] >= LIMIT:
                    eng_sem[eng] = new_sem("e_" + eng)
                    eng_cnt[eng] = 0
                eng_cnt[eng] += 1
                o.sig = (eng_sem[eng], eng_cnt[eng])

    def emit(eng_name, e):
        waited = {}
        for o in P.ops[eng_name]:
            for d in o.deps:
                sem, val = d.sig
                key = id(sem)
                if waited.get(key, 0) < val:
                    e.wait_ge(sem, val)
                    waited[key] = val
            if o.fn is None:
                continue
            ins = o.fn(e)
            if o.sig is not None:
                if o.dma is not None:
                    ins.then_inc(o.sig[0], 16)
                else:
                    ins.then_inc(o.sig[0], 1)

    with sem_stack, nc.Block() as block:
        @block.tensor
        def _(e):
            emit("pe", e)

        @block.scalar
        def _(e):
            emit("act", e)

        @block.vector
        def _(e):
            emit("dve", e)

        @block.gpsimd
        def _(e):
            emit("pool", e)

        @block.sync
        def _(e):
            emit("sp", e)
    build_program.sbuf_left = nc.sbuf_bytes_remaining
    es.close()
    return nc


def _bf(a):
    return np.ascontiguousarray(a, dtype=np.float32)


def prep_shared(inp):
    vecs = np.zeros((128, DEPTH, NV), np.float32)

    def cols(v, n):
        return np.asarray(v, np.float32).reshape(n, 128).T

    for l in range(DEPTH):
        vecs[:, l, V_NMIX:V_NMIX + 8] = cols(inp["norm_mix"][l], 8)
        vecs[:, l, V_NMLP:V_NMLP + 8] = cols(inp["norm_mlp"][l], 8)
        vecs[:, l, V_NPLE:V_NPLE + 8] = cols(inp["norm_ple"][l], 8)
        vecs[:, l, V_CB:V_CB + 4] = cols(inp["conf_dw_b"][l], 4)
        vecs[:, l, V_CLG:V_CLG + 4] = cols(inp["conf_ln_g"][l], 4)
        vecs[:, l, V_CLB:V_CLB + 4] = cols(inp["conf_ln_b"][l], 4)
        vecs[:, l, V_PS:V_PS + 4] = cols(inp["pool_scale"][l], 4)
        vecs[:, l, V_GG:V_GG + 4] = cols(inp["gmlp_ln_g"][l], 4)
        vecs[:, l, V_GB:V_GB + 4] = cols(inp["gmlp_ln_b"][l], 4)
        sc = np.asarray(inp["sc_conv"][l], np.float32)
        vecs[:, l, V_SC:V_SC + 12] = sc.T.reshape(4, 128, 3).transpose(1, 0, 2).reshape(128, 12)
        cw = np.asarray(inp["conf_dw"][l], np.float32)
        vecs[:, l, V_CW:V_CW + 124] = cw.T.reshape(4, 128, 31).transpose(1, 0, 2).reshape(128, 124)
        vecs[:, l, V_NF:V_NF + 8] = cols(inp["norm_final"], 8)
    ws = np.asarray(inp["gmlp_ws"], np.float32)
    wsT = np.ascontiguousarray(ws.transpose(0, 3, 1, 2)).reshape(DEPTH, 128, 512)
    pwt = np.asarray(inp["pool_w"], np.float32)
    poolw = np.ascontiguousarray(pwt.transpose(0, 2, 1, 3)).reshape(DEPTH, 128, 512)
    bs = np.ascontiguousarray(np.asarray(inp["gmlp_bs"], np.float32)).reshape(DEPTH, 512)
    consts = np.zeros((128, 272), np.float32)
    consts[:, 0:128] = np.eye(128, dtype=np.float32)
    consts[:, 128:256] = np.triu(np.ones((128, 128), np.float32))
    consts[:, 256:272] = (1.0 / np.arange(1, 17, dtype=np.float32))[None, :]
    return {
        "w_in": _bf(inp["w_in"]), "w_branch": _bf(inp["w_branch"]), "w_out": _bf(inp["w_out"]),
        "w_up": _bf(inp["w_up"]), "w_down": _bf(inp["w_down"]), "w_ple": _bf(inp["w_ple"]),
        "w_ple_gate": _bf(inp["w_ple_gate"]),
        "vecs": np.ascontiguousarray(vecs.reshape(128, DEPTH * NV)), "wsT": wsT, "poolw": poolw, "bs": bs,
        "consts": consts,
    }


def kernel(**inputs):
    shared = prep_shared(inputs)
    x = np.asarray(inputs["x"], np.float32)
    p = np.asarray(inputs["p"], np.float32)
    B = x.shape[0]
    in_maps = []
    for b in range(B):
        m = dict(shared)
        m["xT"] = np.ascontiguousarray(x[b].T)
        m["pT"] = np.ascontiguousarray(p[:, b].transpose(0, 2, 1))
        in_maps.append(m)
    nc = build_program(DEPTH)
    res = run_bass_kernel_spmd(nc, in_maps, core_ids=list(range(B)))
    out = np.stack([np.ascontiguousarray(res.results[b]["outT"].T) for b in range(B)], axis=0)
    return out.astype(np.float32)
```
